# Optimizing a Trainium2 kernel written in Bass

```python
import math
import jax
import jax.numpy as jnp
from jax import lax
import numpy as np

D_MODEL = 1024
BATCH = 8
SEQ = 2048
DEPTH = 2
DEC_BATCH = 16
DEC_SEQ = 4096
PAST_LEN = 128

HEAD_DIM = 64
A_GROUPS = 4
A_WIDTH = A_GROUPS * HEAD_DIM
CHUNK = 128
B_HEADS = 4
B_WIDTH = B_HEADS * 2 * HEAD_DIM
C_HEADS = 4
C_WIDTH = C_HEADS * HEAD_DIM
MIX_WIDTH = A_WIDTH + B_WIDTH + C_WIDTH
IN_WIDTH = 2 * A_WIDTH + 3 * B_WIDTH + 3 * C_WIDTH
GRID_W = 64
NA_ROWS_MAX = 8
NA_COLS = 16
RPB_ROWS = 2 * NA_ROWS_MAX - 1
RPB_COLS = 2 * NA_COLS - 1
D_FF = ((8 * D_MODEL // 3 + 255) // 256) * 256
DEEPNORM_ALPHA = (2 * DEPTH) ** 0.25
DEEPNORM_BETA = (8 * DEPTH) ** -0.25
LN_EPS = 1e-5
Q_BLOCK = 128

kernel_name = 'hybrid_gmlp_diffattn_natten_encoder'


def layer_norm(x, g, b):
    xf = x.astype(jnp.float32)
    mu = jnp.mean(xf, axis=-1, keepdims=True)
    xc = xf - mu
    var = jnp.mean(jnp.square(xc), axis=-1, keepdims=True)
    return (xc * lax.rsqrt(var + LN_EPS) * g + b).astype(x.dtype)


def rms_norm(x, g):
    xf = x.astype(jnp.float32)
    ms = jnp.mean(jnp.square(xf), axis=-1, keepdims=True)
    return (xf * lax.rsqrt(ms + LN_EPS) * g).astype(x.dtype)


def split_projection(h):
    sizes = [A_WIDTH, A_WIDTH, B_WIDTH, B_WIDTH, B_WIDTH, C_WIDTH, C_WIDTH, C_WIDTH]
    idx = np.cumsum(sizes)[:-1].tolist()
    return jnp.split(h, idx, axis=-1)


def spatial_gating(u, v, ln_g, ln_b, w_s, b_s):
    bsz, s, _ = u.shape
    v = layer_norm(v, ln_g, ln_b)
    vc = v.reshape(bsz, s // CHUNK, CHUNK, A_GROUPS, HEAD_DIM)
    mixed = jnp.einsum('gts,bnsgc->bntgc', w_s, vc) + jnp.transpose(b_s)[None, None, :, :, None]
    return u * mixed.reshape(bsz, s, A_WIDTH)


def differential_attention(q, k, v, lam, lam_init, subln_g):
    bsz, s = q.shape[0], q.shape[1]
    nb = s // Q_BLOCK
    scale = HEAD_DIM ** -0.5
    slopes = jnp.exp2(-8.0 * jnp.arange(1, B_HEADS + 1, dtype=jnp.float32) / B_HEADS)
    kpos = jnp.arange(s)
    qb = jnp.transpose((q * scale).reshape(bsz, nb, Q_BLOCK, B_HEADS, 2, HEAD_DIM), (1, 0, 2, 3, 4, 5))

    def block(args):
        qi, i = args
        sc = jnp.einsum('bqhcd,bkhcd->bchqk', qi, k, preferred_element_type=jnp.float32)
        qpos = i * Q_BLOCK + jnp.arange(Q_BLOCK)
        dist = jnp.abs(qpos[:, None] - kpos[None, :]).astype(jnp.float32)
        p = jax.nn.softmax(sc - slopes[:, None, None] * dist, axis=-1)
        attn = p[:, 0] - lam * p[:, 1]
        return jnp.einsum('bhqk,bkhe->bqhe', attn.astype(v.dtype), v)

    out = lax.map(block, (qb, jnp.arange(nb)))
    out = jnp.transpose(out, (1, 0, 2, 3, 4)).reshape(bsz, s, B_HEADS, 2 * HEAD_DIM)
    out = rms_norm(out, subln_g) * (1.0 - lam_init)
    return out.reshape(bsz, s, B_WIDTH)


def neighbourhood_attention(q, k, v, rpb):
    bsz, s = q.shape[0], q.shape[1]
    rows = s // GRID_W
    wr = min(NA_ROWS_MAX, rows)
    wc = NA_COLS
    scale = HEAD_DIM ** -0.5
    qg = jnp.transpose((q * scale).reshape(bsz, rows, GRID_W, C_HEADS, HEAD_DIM), (1, 0, 2, 3, 4))
    kg = k.reshape(bsz, rows, GRID_W, C_HEADS, HEAD_DIM)
    vg = v.reshape(bsz, rows, GRID_W, C_HEADS, HEAD_DIM)
    col = jnp.arange(GRID_W)
    col_start = jnp.clip(col - wc // 2, 0, GRID_W - wc)
    key_cols = col_start[:, None] + jnp.arange(wc)[None, :]
    col_off = key_cols - col[:, None] + (NA_COLS - 1)

    def row_block(args):
        qr, r = args
        r_start = jnp.clip(r - wr // 2, 0, rows - wr)
        kr = lax.dynamic_slice_in_dim(kg, r_start, wr, axis=1)
        vr = lax.dynamic_slice_in_dim(vg, r_start, wr, axis=1)
        kn = kr[:, :, key_cols]
        vn = vr[:, :, key_cols]
        row_off = r_start + jnp.arange(wr) - r + (NA_ROWS_MAX - 1)
        bias = rpb[:, row_off[:, None, None], col_off[None, :, :]]
        sc = jnp.einsum('bqhd,biqjhd->bhqij', qr, kn, preferred_element_type=jnp.float32)
        sc = sc + jnp.transpose(bias, (0, 2, 1, 3)).astype(jnp.float32)[None]
        p = jax.nn.softmax(sc.reshape(bsz, C_HEADS, GRID_W, wr * wc), axis=-1)
        p = p.reshape(bsz, C_HEADS, GRID_W, wr, wc)
        return jnp.einsum('bhqij,biqjhd->bqhd', p.astype(vn.dtype), vn)

    out = lax.map(row_block, (qg, jnp.arange(rows)))
    return jnp.transpose(out, (1, 0, 2, 3, 4)).reshape(bsz, s, C_WIDTH)


def token_mix(x, layer, w_in, w_out, a_ln_g, a_ln_b, a_w_s, a_b_s, lq1, lk1, lq2, lk2, subln_g, rpb):
    bsz, s, _ = x.shape
    h = x @ w_in
    a_u, a_v, b_q, b_k, b_v, c_q, c_k, c_v = split_projection(h)
    out_a = spatial_gating(jax.nn.gelu(a_u), jax.nn.gelu(a_v), a_ln_g, a_ln_b, a_w_s, a_b_s)
    lam_init = 0.8 - 0.6 * math.exp(-0.3 * layer)
    lam = (jnp.exp(jnp.sum(lq1.astype(jnp.float32) * lk1.astype(jnp.float32)))
           - jnp.exp(jnp.sum(lq2.astype(jnp.float32) * lk2.astype(jnp.float32))) + lam_init)
    out_b = differential_attention(
        b_q.reshape(bsz, s, B_HEADS, 2, HEAD_DIM),
        b_k.reshape(bsz, s, B_HEADS, 2, HEAD_DIM),
        b_v.reshape(bsz, s, B_HEADS, 2 * HEAD_DIM),
        lam, lam_init, subln_g)
    out_c = neighbourhood_attention(
        c_q.reshape(bsz, s, C_HEADS, HEAD_DIM),
        c_k.reshape(bsz, s, C_HEADS, HEAD_DIM),
        c_v.reshape(bsz, s, C_HEADS, HEAD_DIM), rpb)
    return jnp.concatenate([out_a, out_b, out_c], axis=-1) @ w_out


def swiglu(x, w_gate, w_up, w_down):
    return (jax.nn.silu(x @ w_gate) * (x @ w_up)) @ w_down


def trunk(x, w_in, w_out, a_ln_g, a_ln_b, a_w_s, a_b_s, b_lambda_q1, b_lambda_k1, b_lambda_q2,
          b_lambda_k2, b_subln_g, c_rpb, w_gate, w_up, w_down, ln_g, ln_b):
    for l in range(DEPTH):
        mix = token_mix(x, l, w_in[l], w_out[l], a_ln_g[l], a_ln_b[l], a_w_s[l], a_b_s[l],
                        b_lambda_q1[l], b_lambda_k1[l], b_lambda_q2[l], b_lambda_k2[l],
                        b_subln_g[l], c_rpb[l])
        x = layer_norm(DEEPNORM_ALPHA * x + mix, ln_g[l, 0], ln_b[l, 0])
        x = layer_norm(DEEPNORM_ALPHA * x + swiglu(x, w_gate[l], w_up[l], w_down[l]), ln_g[l, 1], ln_b[l, 1])
    return x


def setup_inputs(seed: int = 0) -> dict:
    key = jax.random.key(seed)
    ks = jax.random.split(key, 20)
    f32 = jnp.float32

    def nrm(k, shape, scale):
        return jax.random.normal(k, shape, f32) * scale

    x_prompt = nrm(ks[0], (BATCH, SEQ, D_MODEL), 1.0)
    x_sample = nrm(ks[1], (DEC_BATCH, DEC_SEQ, D_MODEL), 1.0)
    col_scale = jnp.concatenate([
        jnp.ones((2 * A_WIDTH + 2 * B_WIDTH,), f32),
        jnp.full((B_WIDTH,), DEEPNORM_BETA, f32),
        jnp.ones((2 * C_WIDTH,), f32),
        jnp.full((C_WIDTH,), DEEPNORM_BETA, f32)])
    w_in = nrm(ks[2], (DEPTH, D_MODEL, IN_WIDTH), D_MODEL ** -0.5) * col_scale
    w_out = nrm(ks[3], (DEPTH, MIX_WIDTH, D_MODEL), MIX_WIDTH ** -0.5 * DEEPNORM_BETA)
    a_ln_g = 1.0 + nrm(ks[4], (DEPTH, A_WIDTH), 0.02)
    a_ln_b = nrm(ks[5], (DEPTH, A_WIDTH), 0.02)
    a_w_s = nrm(ks[6], (DEPTH, A_GROUPS, CHUNK, CHUNK), CHUNK ** -0.5)
    a_b_s = 1.0 + nrm(ks[7], (DEPTH, A_GROUPS, CHUNK), 0.02)
    b_lambda_q1 = nrm(ks[8], (DEPTH, HEAD_DIM), 0.1)
    b_lambda_k1 = nrm(ks[9], (DEPTH, HEAD_DIM), 0.1)
    b_lambda_q2 = nrm(ks[10], (DEPTH, HEAD_DIM), 0.1)
    b_lambda_k2 = nrm(ks[11], (DEPTH, HEAD_DIM), 0.1)
    b_subln_g = 1.0 + nrm(ks[12], (DEPTH, 2 * HEAD_DIM), 0.02)
    c_rpb = nrm(ks[13], (DEPTH, C_HEADS, RPB_ROWS, RPB_COLS), 0.02)
    w_gate = nrm(ks[14], (DEPTH, D_MODEL, D_FF), D_MODEL ** -0.5 * DEEPNORM_BETA)
    w_up = nrm(ks[15], (DEPTH, D_MODEL, D_FF), D_MODEL ** -0.5 * DEEPNORM_BETA)
    w_down = nrm(ks[16], (DEPTH, D_FF, D_MODEL), D_FF ** -0.5 * DEEPNORM_BETA)
    ln_g = 1.0 + nrm(ks[17], (DEPTH, 2, D_MODEL), 0.02)
    ln_b = nrm(ks[18], (DEPTH, 2, D_MODEL), 0.02)
    return {'x_prompt': x_prompt, 'x_sample': x_sample, 'w_in': w_in, 'w_out': w_out,
            'a_ln_g': a_ln_g, 'a_ln_b': a_ln_b, 'a_w_s': a_w_s, 'a_b_s': a_b_s,
            'b_lambda_q1': b_lambda_q1, 'b_lambda_k1': b_lambda_k1,
            'b_lambda_q2': b_lambda_q2, 'b_lambda_k2': b_lambda_k2,
            'b_subln_g': b_subln_g, 'c_rpb': c_rpb,
            'w_gate': w_gate, 'w_up': w_up, 'w_down': w_down,
            'ln_g': ln_g, 'ln_b': ln_b}


def reference(x_prompt, x_sample, w_in, w_out, a_ln_g, a_ln_b, a_w_s, a_b_s, b_lambda_q1, b_lambda_k1,
              b_lambda_q2, b_lambda_k2, b_subln_g, c_rpb, w_gate, w_up, w_down, ln_g, ln_b):
    y_prompt = trunk(x_prompt, w_in, w_out, a_ln_g, a_ln_b, a_w_s, a_b_s, b_lambda_q1, b_lambda_k1,
                     b_lambda_q2, b_lambda_k2, b_subln_g, c_rpb, w_gate, w_up, w_down, ln_g, ln_b)
    y_sample = trunk(x_sample, w_in, w_out, a_ln_g, a_ln_b, a_w_s, a_b_s, b_lambda_q1, b_lambda_k1,
                     b_lambda_q2, b_lambda_k2, b_subln_g, c_rpb, w_gate, w_up, w_down, ln_g, ln_b)
    return (y_prompt, y_sample)
```

```python
import math
import numpy as np
import concourse.bass as bass
import concourse.mybir as mybir
from concourse.bass_utils import run_bass_kernel_spmd

F32 = mybir.dt.float32
BF16 = mybir.dt.bfloat16
ALU = mybir.AluOpType
AF = mybir.ActivationFunctionType
AX = mybir.AxisListType

D = 1024
DC = 8
INW = 2816
DFF = 2816
NFC = 22
DEPTH = 2
ALPHA = (2 * DEPTH) ** 0.25
EPS = 1e-5
SLOPES = [2.0 ** (-2.0 * (h + 1)) for h in range(4)]
LAM_INIT = [0.8 - 0.6 * math.exp(-0.3 * l) for l in range(DEPTH)]
NEG = -30000.0
OFF_AU, OFF_AV, OFF_BQ, OFF_BK, OFF_BV, OFF_CQ, OFF_CK, OFF_CV = 0, 256, 512, 1024, 1536, 2048, 2304, 2560


class Op:
    __slots__ = ("eng", "fn", "deps", "sig", "sigidx", "is_dma", "slot", "dval")


class Res:
    __slots__ = ("w", "r")

    def __init__(self):
        self.w = None
        self.r = {}


class Prog:
    ENGS = ["pe", "act", "dve", "pool", "sp"]
    NSLOT = 8

    def __init__(self):
        self.ops = {e: [] for e in self.ENGS}
        self.last = {e: None for e in self.ENGS}
        self.slots = {e: [None] * self.NSLOT for e in self.ENGS}
        self.rr = {e: 0 for e in self.ENGS}
        self.cnt = {}
        self.out_dmas = []

    def emit(self, eng, fn, reads=(), writes=(), dma=False, extra=()):
        op = Op()
        op.eng, op.fn, op.sig, op.sigidx, op.is_dma, op.slot, op.dval = eng, fn, False, 0, dma, 0, 0
        deps = set(extra)
        for r in reads:
            if r.w is not None:
                deps.add(r.w)
        for w in writes:
            if w.w is not None:
                deps.add(w.w)
            for o in w.r.values():
                deps.add(o)
        if dma:
            i = self.rr[eng]
            self.rr[eng] = (i + 1) % self.NSLOT
            prev = self.slots[eng][i]
            if prev is not None:
                deps.add(prev)
            self.slots[eng][i] = op
            c = self.cnt.get((eng, i), 0) + 1
            self.cnt[(eng, i)] = c
            op.slot, op.dval = i, 16 * c
        fin = []
        for d in deps:
            if d is op:
                continue
            if (not d.is_dma) and d.eng == eng and eng == "pe":
                continue
            if not d.is_dma:
                d.sig = True
            fin.append(d)
        op.deps = fin
        for r in reads:
            r.r[id(op) if dma else eng] = op
        for w in writes:
            w.w = op
            w.r = {}
        self.ops[eng].append(op)
        if fn is not None and not dma:
            self.last[eng] = op
        return op

    def barrier(self):
        lasts = [o for o in self.last.values() if o is not None]
        dm = [o for e in self.ENGS for o in self.slots[e] if o is not None]
        for e in self.ENGS:
            self.emit(e, None, extra=[o for o in lasts + dm])

    def finalize(self):
        for e in self.ENGS:
            n = 0
            for op in self.ops[e]:
                if op.sig and not op.is_dma:
                    n += 1
                    op.sigidx = n

    def replay(self, eng, e, engsem, dmasem):
        seen = {}
        for op in self.ops[eng]:
            for d in op.deps:
                if d.is_dma:
                    key, sem, val = ("d", d.eng, d.slot), dmasem[d.eng][d.slot], d.dval
                else:
                    key, sem, val = ("e", d.eng), engsem[d.eng], d.sigidx
                if seen.get(key, 0) < val:
                    e.wait_ge(sem, val)
                    seen[key] = val
            if op.fn is not None:
                inst = op.fn(e)
                if op.is_dma:
                    inst.then_inc(dmasem[eng][op.slot], 16)
                elif op.sig:
                    inst.then_inc(engsem[eng], 1)


def na_pattern(S):
    rows = S // 64
    wr = min(8, rows)
    res = []
    ql = np.arange(128)
    for i in range(S // 128):
        r = 2 * i + ql // 64
        c = ql % 64
        rs = np.clip(r - wr // 2, 0, rows - wr)
        cs = np.clip(c - 8, 0, 48)
        t_lo = 64 * rs.min()
        t_hi = 64 * (rs.max() + wr)
        lst = []
        for kc in range(t_lo // 128, (t_hi + 127) // 128):
            k = kc * 128 + np.arange(128)
            kr = (k // 64)[:, None]
            kcol = (k % 64)[:, None]
            valid = (kr >= rs[None]) & (kr < rs[None] + wr) & (kcol >= cs[None]) & (kcol < cs[None] + 16)
            idx = (kr - r[None] + 7) * 31 + (kcol - c[None] + 15)
            idx = np.where(valid, idx, -1).astype(np.int32)
            lst.append((kc, idx))
        res.append(lst)
    return res


def na_tiles(seq_lens):
    uniq = {}
    tiles = []
    plans = {}
    for S in sorted(set(seq_lens)):
        plan = []
        for lst in na_pattern(S):
            pl = []
            for kc, idx in lst:
                key = idx.tobytes()
                if key not in uniq:
                    uniq[key] = len(tiles)
                    tiles.append(idx)
                pl.append((kc, uniq[key]))
            plan.append(pl)
        plans[S] = plan
    return plans, np.stack(tiles)


def host_consts():
    p = np.arange(128, dtype=np.float64)
    bcol = np.zeros((128, 4, 64), np.float32)
    qaug = np.zeros((4, 2, 2, 512), np.float32)
    dtab = np.zeros((4, 128, 4, 512), np.float32)
    ql = np.arange(512)
    for h in range(4):
        s = SLOPES[h]
        for m in range(1, 32):
            bcol[:, h, m] = s * (p - 128 * m)
        for m in range(4, 32):
            bcol[:, h, 32 + m] = -s * (p + 128 * m - 511)
        qaug[h, 0, 0] = -s * (ql & 0xFF)
        qaug[h, 0, 1] = -s * (ql & 0x100)
        b = 511 - ql
        qaug[h, 1, 0] = -s * (b & 0xFF)
        qaug[h, 1, 1] = -s * (b & 0x100)
        for j in range(4):
            dtab[h, :, j, :] = np.exp(-s * np.abs(ql[None, :] - 128 * j - p[:, None]))
    return bcol, qaug, dtab


class _Stop(Exception):
    pass


KSTOP = [None]


MARKS = []


def _chk(tag):
    if KSTOP[0] == tag:
        raise _Stop()


def build(seq_lens, n_tiles):
    NT = sum(seq_lens)
    plans, _ = na_tiles(seq_lens)
    nc = bass.Bass("TRN2", target_bir_lowering=False)
    P = Prog()

    def din(name, shape):
        return nc.dram_tensor(name, list(shape), F32, kind="ExternalInput").ap()

    x_d = din("x", [NT, D])
    w_in_d = din("w_in", [DEPTH, D, INW])
    w_out_d = din("w_out", [DEPTH, D, D])
    w_gate_d = din("w_gate", [DEPTH, D, DFF])
    w_up_d = din("w_up", [DEPTH, D, DFF])
    w_down_d = din("w_down", [DEPTH, DFF, D])
    algb_d = din("algb", [DEPTH, 128, 2, 512])
    wsT_d = din("wsT", [DEPTH, 128, 4, 128])
    bs_d = din("bs", [DEPTH, 1, 512])
    lamp_d = din("lamp", [DEPTH, 128, 4, 64])
    subg_d = din("subg", [DEPTH, 128, 1])
    cb_d = din("cb", [DEPTH, 128, n_tiles, 4, 128])
    lnp_d = din("lnp", [DEPTH, 2, 128, 2, D])
    bcol_d = din("bcol", [128, 4, 64])
    qaug_d = din("qaug", [4, 2, 2, 512])
    dtab_d = din("dtab", [4, 128, 4, 512])
    ident_d = din("ident", [128, 128])
    y_d = nc.dram_tensor("y", [NT, D], F32, kind="ExternalOutput").ap()

    def dscr(name, shape, dt):
        return nc.dram_tensor(name, list(shape), dt, kind="Internal").ap()

    xa_d = dscr("xa_s", [NT, D], F32)
    xb_d = dscr("xb_s", [NT, D], F32)
    win_b = dscr("win_b", [DEPTH, 128, DC, INW], BF16)
    wg_b = dscr("wg_b", [DEPTH, 128, DC, DFF], BF16)
    wu_b = dscr("wu_b", [DEPTH, 128, DC, DFF], BF16)
    wo_b = dscr("wo_b", [DEPTH, 128, DC, D], BF16)
    wd_b = dscr("wd_b", [DEPTH, 128, NFC, D], BF16)
    cb_b = dscr("cb_b", [DEPTH, 128, n_tiles * 512], BF16)
    qaug_b = dscr("qaug_b", [4, 2, 2, 512], BF16)
    SMAX = max(seq_lens)
    mix_b = dscr("mix_b", [128, 8, SMAX], BF16)

    from contextlib import ExitStack
    es = ExitStack()

    def sb(name, shape, dt):
        return es.enter_context(nc.sbuf_tensor("s_" + name, list(shape), dt))

    REG = 34816
    xT = sb("xT", [128, DC, SMAX], BF16)
    reg = sb("reg", [128, REG], BF16)
    wring = [sb(f"wring{i}", [128, DC, 512], BF16) for i in range(3)]
    ident_f = sb("ident_f", [128, 128], F32)
    ident_bf = sb("ident_bf", [128, 128], BF16)
    ones_bf = sb("ones_bf", [128, 128], BF16)
    ones64 = sb("ones64", [64, 128], BF16)
    mhalf = sb("mhalf", [128, 512], F32)
    bcol = sb("bcol", [128, 4, 64], F32)
    wsT = sb("wsT_sb", [128, DEPTH, 4, 128], BF16)
    bsr = sb("bsr", [64, DEPTH, 512], BF16)
    neglam = sb("neglam", [128, DEPTH], F32)
    gsub = sb("gsub", [128, DEPTH], F32)
    fA = [sb(f"fA{i}", [128, 1024], F32) for i in range(6)]
    xres = [sb(f"xres{i}", [128, 1024], F32) for i in range(2)]
    lnp = sb("lnp_sb", [128, 2, D], F32)
    st6 = sb("st6", [128, 4, 6], F32)
    st2 = sb("st2", [128, 8], F32)
    xbf = reg[:, 8192:9216]
    ps = [es.enter_context(nc.psum_tensor(f"ps{i}", [128, 512], F32)) for i in range(8)]

    R = lambda: Res()
    r_xT = [R() for _ in range(SMAX // 512)]
    r_reg = R()
    r_wring = [[R(), R(), R()] for _ in range(3)]
    r_ps = [R() for _ in range(8)]
    r_fA = [R() for _ in range(6)]
    r_xres = [R(), R()]
    r_c = R()
    r_lnp, r_st, r_xbf = R(), R(), R()
    r_dram = R()

    def MM(out, lhsT, rhs, start, stop, reads, writes):
        return P.emit("pe", lambda e: e.matmul(out, lhsT, rhs, start=start, stop=stop, skip_group_check=True), reads, writes)

    def ACTV(out, in_, func, reads, writes, bias=0.0, scale=1.0):
        return P.emit("act", lambda e: e.activation(out, in_, func, bias=bias, scale=scale), reads, writes)

    def TS(eng, out, in0, s1, s2, op0, op1, reads, writes):
        return P.emit(eng, lambda e: e.tensor_scalar(out, in0, s1, s2, op0, op1), reads, writes)

    def TT(eng, out, in0, in1, op, reads, writes):
        return P.emit(eng, lambda e: e.tensor_tensor(out, in0, in1, op), reads, writes)

    def STT(eng, out, in0, sc, in1, op0, op1, reads, writes):
        return P.emit(eng, lambda e: e.scalar_tensor_tensor(out, in0, sc, in1, op0, op1), reads, writes)

    def CP(eng, out, in_, reads, writes):
        if eng == "act":
            return P.emit("act", lambda e: e.copy(out, in_), reads, writes)
        return P.emit(eng, lambda e: e.tensor_copy(out, in_), reads, writes)

    def DMA(q, out, in_, reads, writes):
        return P.emit(q, lambda e: e.dma_start(out=out, in_=in_), reads, writes, dma=True)

    def MEMSET(eng, ap, val, writes):
        return P.emit(eng, lambda e: e.memset(ap, val), (), writes)

    bank_rr = [0]

    def nextbank(pool):
        b = pool[bank_rr[0] % len(pool)]
        bank_rr[0] += 1
        return b

    try:
        stg_f = [fA[0], fA[1], fA[2], fA[3], fA[4], fA[5], xres[0], xres[1]]
        stg_r = r_fA + r_xres
        def conv(src, dst, n_inner, k):
            a, b = src.shape[1], src.shape[2]
            per = max(1, 1024 // b)
            i = 0
            while i < a:
                na = min(per, a - i)
                slot = k[0] % 8
                k[0] += 1
                f = stg_f[slot][:, 0:na * b].rearrange("p (a b) -> p a b", b=b)
                g = reg[:, slot * 1024: slot * 1024 + na * b].rearrange("p (a b) -> p a b", b=b)
                DMA("sp", f, src[:, i:i + na, :], [], [stg_r[slot]])
                eng = ["dve", "act"][k[0] % 2]
                CP(eng, g, f, [stg_r[slot]], [r_st_slots[slot]])
                DMA("pool", dst[:, i:i + na, :], g, [r_st_slots[slot]], [R()])
                i += na

        r_st_slots = [R() for _ in range(8)]
        kk = [0]
        for l in range(DEPTH):
            for (srcw, dstw, ncol) in ((w_in_d, win_b, INW), (w_gate_d, wg_b, DFF), (w_up_d, wu_b, DFF), (w_out_d, wo_b, D)):
                s3 = srcw[l].rearrange("(dc p) n -> p dc n", p=128)
                for c0 in range(0, ncol, 512):
                    c1 = min(ncol, c0 + 512)
                    conv(s3[:, :, c0:c1], dstw[l][:, :, c0:c1], c1 - c0, kk)
            s3 = w_down_d[l].rearrange("(fc p) n -> p fc n", p=128)
            conv(s3, wd_b[l], D, kk)
            cbs = cb_d[l].rearrange("p t h q -> p t (h q)")
            conv(cbs, cb_b[l].rearrange("p (t x) -> p t x", x=512), 512, kk)
        DMA("sp", ident_f[:], ident_d, [], [r_c])
        CP("dve", ident_bf[:], ident_f[:], [r_c], [r_c])
        MEMSET("dve", ones_bf[:], 1.0, [r_c])
        MEMSET("dve", ones64[:], 0.0, [r_c])
        MEMSET("dve", ones64[0:1, :], 1.0, [r_c])
        MEMSET("dve", ones64[32:33, :], 1.0, [r_c])
        MEMSET("pool", mhalf[:], -0.5, [r_c])
        DMA("sp", bcol[:], bcol_d, [], [r_c])
        qa_f = fA[4][0:16, 0:512]
        DMA("sp", qa_f, qaug_d.rearrange("h a r q -> (h a r) q"), [], [r_fA[4]])
        qa_b = xbf[0:16, 0:512]
        CP("dve", qa_b, qa_f, [r_fA[4]], [r_xbf])
        DMA("pool", qaug_b.rearrange("h a r q -> (h a r) q"), qa_b, [r_xbf], [R()])
        for l in range(DEPTH):
            t = fA[5][:, 0:512].rearrange("p (g t) -> p g t", t=128)
            DMA("sp", t, wsT_d[l], [], [r_fA[5]])
            CP("dve", wsT[:, l, :, :], t, [r_fA[5]], [r_c])
            bt = fA[4][0:64, 0:512]
            MEMSET("dve", bt, 0.0, [r_fA[4]])
            DMA("sp", fA[4][0:1, 0:512], bs_d[l], [], [r_fA[4]])
            DMA("sp", fA[4][32:33, 0:512], bs_d[l], [], [r_fA[4]])
            hi = xbf[0:64, 0:512]
            lo = xbf[0:64, 512:1024]
            CP("dve", hi, bt, [r_fA[4]], [r_xbf])
            tmp = fA[3][0:64, 0:512]
            TT("dve", tmp, bt, hi, ALU.subtract, [r_fA[4], r_xbf], [r_fA[3]])
            CP("dve", lo, tmp, [r_fA[3]], [r_xbf])
            MEMSET("dve", bsr[:, l, :], 0.0, [r_c])
            CP("dve", bsr[0:1, l, :], xbf[0:1, 0:512], [r_xbf], [r_c])
            CP("dve", bsr[32:33, l, :], xbf[32:33, 512:1024], [r_xbf], [r_c])
            lp = fA[5][:, 0:256].rearrange("p (a b) -> p a b", b=64)
            DMA("sp", lp, lamp_d[l], [], [r_fA[5]])
            pr = fA[3][:, 0:128].rearrange("p (a b) -> p a b", b=64)
            TT("dve", pr[:, 0, :], lp[:, 0, :], lp[:, 1, :], ALU.mult, [r_fA[5]], [r_fA[3]])
            TT("dve", pr[:, 1, :], lp[:, 2, :], lp[:, 3, :], ALU.mult, [r_fA[5]], [r_fA[3]])
            P.emit("dve", lambda e, pr=pr: e.reduce_sum(st2[:, 0:1], pr[:, 0, :], AX.X), [r_fA[3]], [r_st])
            P.emit("dve", lambda e, pr=pr: e.reduce_sum(st2[:, 1:2], pr[:, 1, :], AX.X), [r_fA[3]], [r_st])
            ACTV(st2[:, 2:4], st2[:, 0:2], AF.Exp, [r_st], [r_st])
            TT("dve", st2[:, 4:5], st2[:, 3:4], st2[:, 2:3], ALU.subtract, [r_st], [r_st])
            TS("dve", neglam[:, l:l + 1], st2[:, 4:5], -LAM_INIT[l], None, ALU.add, ALU.bypass, [r_st], [r_c])
            DMA("sp", st2[:, 5:6], subg_d[l], [], [r_st])
            TS("dve", gsub[:, l:l + 1], st2[:, 5:6], 1.0 - LAM_INIT[l], None, ALU.mult, ALU.bypass, [r_st], [r_c])
        P.barrier()
        MARKS.append(('prologue', sum(1 for o_ in P.ops['pe'] if o_.fn is not None)))
        _chk('prologue')

        def transpose_to_xT(src_f32, src_res, t0):
            for half in range(2):
                b = nextbank(list(range(8)))
                for i in range(4):
                    dc = half * 4 + i
                    P.emit("pe", lambda e, b=b, i=i, dc=dc: e.transpose(ps[b][:, i * 128:(i + 1) * 128], src_f32[:, dc * 128:(dc + 1) * 128], ident_f[:]),
                           [src_res, r_c], [r_ps[b]])
                CP("dve" if half == 0 else "act", xT[:, half * 4:half * 4 + 4, t0:t0 + 128],
                   ps[b][:].rearrange("p (a t) -> p a t", t=128), [r_ps[b]], [r_xT[t0 // 512]])

        def ln_epilogue(ybanks, xr, xr_res, out_dram_rows, t0, do_T, zi, oi):
            z, zr = fA[zi], r_fA[zi]
            o, orr = fA[oi], r_fA[oi]
            for hf in range(2):
                STT("dve", z[:, hf * 512:(hf + 1) * 512], xr[:, hf * 512:(hf + 1) * 512], ALPHA, ps[ybanks[hf]][:],
                    ALU.mult, ALU.add, [xr_res, r_ps[ybanks[hf]]], [zr])
            for hf in range(2):
                P.emit("dve", lambda e, hf=hf: e.bn_stats(st6[:, hf, :], z[:, hf * 512:(hf + 1) * 512]), [zr], [r_st])
            P.emit("dve", lambda e: e.bn_aggr(st2[:, 0:2], st6[:, 0:2, :].rearrange("p a b -> p (a b)")), [r_st], [r_st])
            TS("dve", st2[:, 2:3], st2[:, 1:2], EPS, None, ALU.add, ALU.bypass, [r_st], [r_st])
            TT("pool", st2[:, 3:4], st2[:, 2:3], mhalf[:, 0:1], ALU.pow, [r_st, r_c], [r_st])
            STT("dve", st2[:, 4:5], st2[:, 0:1], -1.0, st2[:, 3:4], ALU.mult, ALU.mult, [r_st], [r_st])
            ACTV(z[:], z[:], AF.Identity, [zr, r_st], [zr], bias=st2[:, 4:5], scale=st2[:, 3:4])
            TT("dve", z[:], z[:], lnp[:, 0, :], ALU.mult, [zr, r_lnp], [zr])
            TT("dve", o[:], z[:], lnp[:, 1, :], ALU.add, [zr, r_lnp], [orr])
            DMA("pool", out_dram_rows, o[:], [orr], [R()])
            if do_T:
                return lambda: transpose_to_xT(o, orr, t0)
            return None

        tok0 = 0
        for si, S in enumerate(seq_lens):
            NTT = S // 512
            NKC = S // 128
            for t in range(NKC):
                xb4 = [xres[0], xres[1], fA[0], fA[1]][t % 4]
                rb4 = [r_xres[0], r_xres[1], r_fA[0], r_fA[1]][t % 4]
                DMA("sp", xb4[:], x_d[tok0 + t * 128: tok0 + (t + 1) * 128, :], [], [rb4])
                transpose_to_xT(xb4, rb4, t * 128)
            P.barrier()
            MARKS.append(('x0', sum(1 for o_ in P.ops['pe'] if o_.fn is not None)))
            _chk('x0')

            for l in range(DEPTH):
                xin_d = x_d if l == 0 else xb_d
                last = (l == DEPTH - 1)
                xout_d = y_d if last else xb_d
                wA = wring[0]
                DMA("sp", wA[:], win_b[l][:, :, 0:512], [], [r_wring[0][0]])
                algb = fA[5][:].rearrange("p (a b) -> p a b", b=512)
                r_algb = r_fA[5]
                DMA("sp", algb, algb_d[l], [], [r_algb])
                uT2 = [reg[:, i * 1024:(i + 1) * 1024].rearrange("p (j t) -> p j t", t=512) for i in range(2)]
                vln2 = [reg[:, 2048 + i * 1024:2048 + (i + 1) * 1024].rearrange("p (s c) -> p s c", c=256) for i in range(2)]
                oa = reg[:, 4096:5120].rearrange("p (j t) -> p j t", t=512)
                r_uT2, r_vln2, r_oa = [R(), R()], [R(), R()], R()

                def gelu_chain(src_ps, bank, width, dst, dst_res, fa, fb):
                    xs = fA[fa][:, 0:width]
                    tt_ = fA[fb][:, 0:width]
                    ACTV(xs, src_ps, AF.Identity, [r_ps[bank]], [r_fA[fa]], scale=0.5)
                    ACTV(tt_, src_ps, AF.Square, [r_ps[bank]], [r_fA[fb]], scale=0.5)
                    STT("dve", tt_, tt_, 1.0 / 0.17886, xs, ALU.add, ALU.mult, [r_fA[fb], r_fA[fa]], [r_fA[fb]])
                    ACTV(tt_, tt_, AF.Tanh, [r_fA[fb]], [r_fA[fb]], scale=1.5957691216 * 0.17886)
                    STT("dve", dst, tt_, 1.0, xs, ALU.add, ALU.mult, [r_fA[fb], r_fA[fa]], [dst_res])

                def a_proj(tt):
                    T0 = tt * 512
                    uT, vln, r_uT, r_vln = uT2[tt % 2], vln2[tt % 2], r_uT2[tt % 2], r_vln2[tt % 2]
                    for j in range(2):
                        b = nextbank(list(range(8)))
                        for dc in range(DC):
                            MM(ps[b][:], wA[:, dc, OFF_AU + j * 128: OFF_AU + (j + 1) * 128], xT[:, dc, T0:T0 + 512],
                               dc == 0, dc == DC - 1, [r_wring[0][0], r_xT[tt]], [r_ps[b]])
                        gelu_chain(ps[b][:], b, 512, uT[:, j, :], r_uT, j * 2, j * 2 + 1)
                    for sp2 in range(2):
                        b = nextbank(list(range(8)))
                        for s_ in range(2):
                            sub = sp2 * 2 + s_
                            for dc in range(DC):
                                MM(ps[b][:, s_ * 256:(s_ + 1) * 256], xT[:, dc, T0 + sub * 128: T0 + (sub + 1) * 128],
                                   wA[:, dc, OFF_AV:OFF_AV + 256], (dc == 0 and s_ == 0), dc == DC - 1,
                                   [r_wring[0][0], r_xT[tt]], [r_ps[b]])
                        vg = fA[4][:, 0:512]
                        gelu_chain(ps[b][:], b, 512, vg, r_fA[4], 0 + sp2 * 2, 1 + sp2 * 2)
                        for s_ in range(2):
                            P.emit("dve", lambda e, s_=s_, vg=vg: e.bn_stats(st6[:, s_, :], vg[:, s_ * 256:(s_ + 1) * 256]), [r_fA[4]], [r_st])
                            P.emit("dve", lambda e, s_=s_: e.bn_aggr(st2[:, 0:2], st6[:, s_, :]), [r_st], [r_st])
                            TS("dve", st2[:, 2:3], st2[:, 1:2], EPS, None, ALU.add, ALU.bypass, [r_st], [r_st])
                            TT("pool", st2[:, 3:4], st2[:, 2:3], mhalf[:, 0:1], ALU.pow, [r_st, r_c], [r_st])
                            TS("dve", vg[:, s_ * 256:(s_ + 1) * 256], vg[:, s_ * 256:(s_ + 1) * 256], st2[:, 0:1], st2[:, 3:4],
                               ALU.subtract, ALU.mult, [r_fA[4], r_st], [r_fA[4]])
                        TT("dve", vg, vg, algb[:, 0, :], ALU.mult, [r_fA[4], r_algb], [r_fA[4]])
                        TT("dve", vln[:, sp2 * 2:sp2 * 2 + 2, :], vg.rearrange("p (s c) -> p s c", c=256), algb[:, 1, :].rearrange("p (s c) -> p s c", c=256),
                           ALU.add, [r_fA[4], r_algb], [r_vln])
                def a_mix(tt):
                    T0 = tt * 512
                    uT, vln, r_uT, r_vln = uT2[tt % 2], vln2[tt % 2], r_uT2[tt % 2], r_vln2[tt % 2]
                    for sub in range(4):
                        b = nextbank(list(range(8)))
                        first = True
                        for ab in range(2):
                            for j in range(2):
                                g = 2 * j + ab
                                col = (ab * 2 + j) * 128
                                MM(ps[b][:, col:col + 128], vln[:, sub, j * 128:(j + 1) * 128], wsT[:, l, g, :], first, False,
                                   [r_vln, r_c], [r_ps[b]])
                                first = False
                                MM(ps[b][:, col:col + 128], ones64[:, :], bsr[:, l, g * 128:(g + 1) * 128], False, True,
                                   [r_c], [r_ps[b]])
                        for ab in range(2):
                            pa = slice(ab * 64, ab * 64 + 64)
                            TT("dve", oa[pa, :, sub * 128:(sub + 1) * 128],
                               ps[b][pa, ab * 256:(ab + 1) * 256].rearrange("p (j t) -> p j t", t=128),
                               uT[pa, :, sub * 128:(sub + 1) * 128], ALU.mult, [r_ps[b], r_uT], [r_oa])
                    DMA("pool", mix_b[:, 0:2, T0:T0 + 512], oa, [r_oa], [R()])
                a_proj(0)
                for tt in range(NTT):
                    if tt + 1 < NTT:
                        a_proj(tt + 1)
                    a_mix(tt)
                P.barrier()
                MARKS.append(('A', sum(1 for o_ in P.ops['pe'] if o_.fn is not None)))
                _chk('A')

                cqT = reg[:, 0:2 * S].rearrange("p (j t) -> p j t", t=S)
                ckT = reg[:, 2 * S:4 * S].rearrange("p (j t) -> p j t", t=S)
                cv = reg[:, 4 * S:6 * S].rearrange("p (k c) -> p k c", c=256)
                cbt = reg[:, 6 * S:6 * S + n_tiles * 512].rearrange("p (t x) -> p t x", x=512)
                coff = 6 * S + n_tiles * 512
                PTc = [reg[:, coff + i * 512: coff + (i + 1) * 512] for i in range(3)]
                oc = reg[:, coff + 1536: coff + 2560].rearrange("p (j t) -> p j t", t=512)
                cqm = [reg[:, coff + 2560 + i * 512: coff + 3072 + i * 512].rearrange("p (h t) -> p h t", t=128) for i in range(2)]
                r_cqm = [R(), R()]
                assert coff + 3584 <= REG
                MEMSET("dve", reg[:, coff + 2560: coff + 3584], 0.0, [r_cqm[0], r_cqm[1]])
                r_cq, r_ck, r_cv, r_cbt, r_oc = [R() for _ in range(NTT)], [R() for _ in range(NTT)], [R() for _ in range(NTT)], R(), R()
                r_PTc = [R(), R(), R()]
                DMA("sp", wring[1][:], win_b[l][:, :, OFF_CQ:OFF_CQ + 512], [], [r_wring[1][0]])
                DMA("sp", wring[2][:, :, 0:256], win_b[l][:, :, OFF_CV:OFF_CV + 256], [], [r_wring[2][0]])
                DMA("sp", cbt, cb_b[l].rearrange("p (t x) -> p t x", x=512), [], [r_cbt])
                for tt in range(NTT):
                    T0 = tt * 512
                    for (dst, rr_, off, scl) in ((cqT, r_cq, 0, 0.125), (ckT, r_ck, 256, 1.0)):
                        for j in range(2):
                            b = nextbank(list(range(8)))
                            for dc in range(DC):
                                MM(ps[b][:], wring[1][:, dc, off + j * 128: off + (j + 1) * 128], xT[:, dc, T0:T0 + 512],
                                   dc == 0, dc == DC - 1, [r_wring[1][0], r_xT[tt]], [r_ps[b]])
                            if j == 0:
                                ACTV(dst[:, j, T0:T0 + 512], ps[b][:], AF.Identity, [r_ps[b]], [rr_[tt]], scale=scl)
                            else:
                                TS("dve", dst[:, j, T0:T0 + 512], ps[b][:], scl, None, ALU.mult, ALU.bypass, [r_ps[b]], [rr_[tt]])
                    for sp2 in range(2):
                        b = nextbank(list(range(8)))
                        for s_ in range(2):
                            sub = sp2 * 2 + s_
                            for dc in range(DC):
                                MM(ps[b][:, s_ * 256:(s_ + 1) * 256], xT[:, dc, T0 + sub * 128:T0 + (sub + 1) * 128],
                                   wring[2][:, dc, 0:256], (dc == 0 and s_ == 0), dc == DC - 1, [r_wring[2][0], r_xT[tt]], [r_ps[b]])
                        CP("act" if sp2 == 0 else "dve", cv[:, tt * 4 + sp2 * 2: tt * 4 + sp2 * 2 + 2, :],
                           ps[b][:].rearrange("p (s c) -> p s c", c=256), [r_ps[b]], [r_cv[tt]])
                plan = plans[S]
                r_cqm4 = [[R() for _ in range(4)] for _ in range(2)]

                def c_cqm(i):
                    for h in range(4):
                        j, bb = h // 2, h % 2
                        pa = slice(bb * 64, bb * 64 + 64)
                        CP("dve" if h % 2 == 0 else "act", cqm[i % 2][pa, h, :], cqT[pa, j, i * 128:(i + 1) * 128], [r_cq[i // 4]], [r_cqm4[i % 2][h]])

                def c_qk(i, n):
                    kc, tid = plan[i][n]
                    b = nextbank([0, 1, 2, 3])
                    MM(ps[b][:], ident_bf[:], cbt[:, tid, :], True, False, [r_c, r_cbt], [r_ps[b]])
                    for j in range(2):
                        MM(ps[b][:, j * 256:(j + 1) * 256], ckT[:, j, kc * 128:(kc + 1) * 128],
                           cqm[i % 2][:, 2 * j:2 * j + 2, :].rearrange("p h t -> p (h t)"),
                           False, True, [r_ck[kc // 4], r_cqm4[i % 2][2 * j], r_cqm4[i % 2][2 * j + 1]], [r_ps[b]])
                    return b

                pi = [0]

                def c_rest(i, n, b):
                    kc, tid = plan[i][n]
                    lst = plan[i]
                    b_out, b_z = 4 + 2 * (i % 2), 5 + 2 * (i % 2)
                    pt = pi[0] % 3
                    pi[0] += 1
                    ACTV(PTc[pt], ps[b][:], AF.Exp, [r_ps[b]], [r_PTc[pt]])
                    for j in range(2):
                        MM(ps[b_out][:, j * 256:(j + 1) * 256], cv[:, kc, j * 128:(j + 1) * 128], PTc[pt][:, j * 256:(j + 1) * 256],
                           (n == 0 and j == 0), n == len(lst) - 1, [r_cv[kc // 4], r_PTc[pt]], [r_ps[b_out]])
                    MM(ps[b_z][:], ones_bf[:], PTc[pt], n == 0, n == len(lst) - 1, [r_c, r_PTc[pt]], [r_ps[b_z]])
                    if n == len(lst) - 1:
                        tq = i // 4
                        rz = fA[i % 2][:, 0:512]
                        P.emit("dve", lambda e, rz=rz, b_z=b_z: e.reciprocal(rz, ps[b_z][:]), [r_ps[b_z]], [r_fA[i % 2]])
                        for h in range(4):
                            j, bb = h // 2, h % 2
                            pa = slice(bb * 64, bb * 64 + 64)
                            TT("dve", oc[pa, j, (i % 4) * 128:(i % 4 + 1) * 128], ps[b_out][pa, h * 128:(h + 1) * 128], rz[pa, h * 128:(h + 1) * 128],
                               ALU.mult, [r_ps[b_out], r_fA[i % 2]], [r_oc])
                        if i % 4 == 3:
                            DMA("pool", mix_b[:, 6:8, tq * 512:(tq + 1) * 512], oc, [r_oc], [R()])

                items = [(i, n) for i in range(NKC) for n in range(len(plan[i]))]
                c_cqm(0)
                if NKC > 1:
                    c_cqm(1)
                prevb = c_qk(*items[0])
                for ix, (i, n) in enumerate(items):
                    nxtb = c_qk(*items[ix + 1]) if ix + 1 < len(items) else None
                    c_rest(i, n, prevb)
                    prevb = nxtb
                    if n == len(plan[i]) - 1 and i + 2 < NKC:
                        c_cqm(i + 2)
                P.barrier()
                MARKS.append(('C', sum(1 for o_ in P.ops['pe'] if o_.fn is not None)))
                _chk('C')

                kTm = [reg[:, 0:S], reg[:, S:2 * S]]
                vB = reg[:, 2 * S:3 * S].rearrange("p (k e) -> p k e", e=128)
                qTa = reg[:, 3 * S:4 * S]
                o_ = 4 * S
                qv = [{}, {}]
                for par in range(2):
                    for ti, ty in enumerate(("L", "R", "N")):
                        for c in range(2):
                            qv[par][(ty, c)] = reg[:, o_:o_ + 512]
                            o_ += 512
                PT = []
                for i in range(6):
                    PT.append(reg[:, o_:o_ + 512])
                    o_ += 512
                ob = reg[:, o_:o_ + 512]
                o_ += 512
                sqb = reg[:, o_:o_ + 512]
                o_ += 512
                assert o_ <= REG
                r_k, r_v = [R() for _ in range(NTT)], [R() for _ in range(NTT)]
                r_PT, r_ob, r_sq, r_dt = [R() for _ in range(6)], R(), R(), R()
                r_qv = [{(ty, c): R() for ty in ("L", "R", "N") for c in range(2)} for _ in range(2)]
                r_qT = [R() for _ in range(NTT)]
                MEMSET("dve", reg[:, 0:2 * S], 0.0, [r_reg])
                MEMSET("dve", reg[64:66, 0:S], 1.0, [r_reg])
                MEMSET("dve", reg[0:2, S:2 * S], 1.0, [r_reg])
                MEMSET("dve", reg[:, 4 * S:4 * S + 12 * 512], 0.0, [r_reg])
                P.barrier()
                dt_sb = [fA[4], fA[5]]
                pending = [None]
                r_pf = [R(), R()]
                for h in range(4):
                    wB = wring[h % 2]
                    rwBs = r_wring[h % 2]
                    for n_, off in enumerate((OFF_BQ, OFF_BK, OFF_BV)):
                        DMA("sp", wB[:, :, n_ * 128:(n_ + 1) * 128], win_b[l][:, :, off + h * 128: off + (h + 1) * 128], [], [rwBs[n_]])
                    for jj in range(2):
                        DMA("sp", dt_sb[jj][:].rearrange("p (a b) -> p a b", b=512), dtab_d[h][:, jj * 2:jj * 2 + 2, :], [], [r_fA[4 + jj]])
                    for par in range(2):
                        for ai, ty in enumerate(("L", "R")):
                            DMA("sp", qv[par][(ty, 0)][64:66, :], qaug_b[h, ai], [], [r_qv[par][(ty, 0)]])
                            DMA("sp", qv[par][(ty, 1)][0:2, :], qaug_b[h, ai], [], [r_qv[par][(ty, 1)]])
                    for tt in range(NTT):
                        T0 = tt * 512
                        b = nextbank([0, 1, 2, 3])
                        for dc in range(DC):
                            MM(ps[b][:], wB[:, dc, 128:256], xT[:, dc, T0:T0 + 512], dc == 0, dc == DC - 1, [rwBs[1], r_xT[tt]], [r_ps[b]])
                        CP("act", kTm[0][0:64, T0:T0 + 512], ps[b][0:64, :], [r_ps[b]], [r_k[tt]])
                        CP("dve", kTm[1][64:128, T0:T0 + 512], ps[b][64:128, :], [r_ps[b]], [r_k[tt]])
                        b = nextbank([0, 1, 2, 3])
                        for sub in range(4):
                            for dc in range(DC):
                                MM(ps[b][:, sub * 128:(sub + 1) * 128], xT[:, dc, T0 + sub * 128:T0 + (sub + 1) * 128], wB[:, dc, 256:384],
                                   (dc == 0 and sub == 0), dc == DC - 1, [rwBs[2], r_xT[tt]], [r_ps[b]])
                        CP("dve", vB[:, tt * 4:tt * 4 + 4, :], ps[b][:].rearrange("p (s e) -> p s e", e=128), [r_ps[b]], [r_v[tt]])
                        b = nextbank([0, 1, 2, 3])
                        for dc in range(DC):
                            MM(ps[b][:], wB[:, dc, 0:128], xT[:, dc, T0:T0 + 512], dc == 0, dc == DC - 1, [rwBs[0], r_xT[tt]], [r_ps[b]])
                        ACTV(qTa[:, T0:T0 + 512], ps[b][:], AF.Identity, [r_ps[b]], [r_qT[tt]], scale=0.125)
                        if tt == min(1, NTT - 1) and pending[0] is not None:
                            pending[0](nextbank([0, 1, 2, 3]))
                            pending[0] = None
                    pti = 0

                    def qproj(qb_, b_):
                        par_ = qb_ % 2
                        types = ["N"] + (["L"] if qb_ > 0 else []) + (["R"] if qb_ < NTT - 1 else [])
                        for n_, ty in enumerate(types):
                            CP("dve", qv[par_][(ty, 0)][0:64, :], qTa[0:64, qb_ * 512:(qb_ + 1) * 512], [r_qT[qb_]], [r_qv[par_][(ty, 0)]])
                            CP("pool", qv[par_][(ty, 1)][64:128, :], qTa[64:128, qb_ * 512:(qb_ + 1) * 512], [r_qT[qb_]], [r_qv[par_][(ty, 1)]])

                    qproj(0, 3)
                    for qb in range(NTT):
                        Q0 = qb * 512
                        par = qb % 2
                        bo = [4, 5]
                        bz = [6, 7]

                        def qk(kc, pos):
                            if kc < 4 * qb:
                                ty = "L"
                            elif kc < 4 * qb + 4:
                                ty = "N"
                            else:
                                ty = "R"
                            bs_ = []
                            for c in range(2):
                                b2 = 2 * (pos % 2) + c
                                MM(ps[b2][:], kTm[c][:, kc * 128:(kc + 1) * 128], qv[par][(ty, c)], True, True, [r_k[kc // 4], r_qv[par][(ty, c)]], [r_ps[b2]])
                                bs_.append(b2)
                            return bs_

                        def rest(kc, bs_, pti, pos):
                            for c in range(2):
                                pt = (pti + c) % 6
                                if kc < 4 * qb:
                                    m = 4 * qb - kc
                                    ACTV(PT[pt], ps[bs_[c]][:], AF.Exp, [r_ps[bs_[c]], r_c], [r_PT[pt]], bias=bcol[:, h, m:m + 1])
                                elif kc >= 4 * qb + 4:
                                    m = kc - 4 * qb
                                    ACTV(PT[pt], ps[bs_[c]][:], AF.Exp, [r_ps[bs_[c]], r_c], [r_PT[pt]], bias=bcol[:, h, 32 + m:33 + m])
                                else:
                                    jd = kc - 4 * qb
                                    pf = fA[c][:, 0:512]
                                    ACTV(pf, ps[bs_[c]][:], AF.Exp, [r_ps[bs_[c]]], [r_pf[c]])
                                    TT("dve", PT[pt], pf, dt_sb[jd // 2][:, (jd % 2) * 512:(jd % 2 + 1) * 512], ALU.mult,
                                       [r_pf[c], r_fA[4 + jd // 2]], [r_PT[pt]])
                                MM(ps[bo[c]][:], vB[:, kc, :], PT[pt], pos == 0, pos == NKC - 1, [r_v[kc // 4], r_PT[pt]], [r_ps[bo[c]]])
                                MM(ps[bz[c]][:], ones_bf[:], PT[pt], pos == 0, pos == NKC - 1, [r_c, r_PT[pt]], [r_ps[bz[c]]])

                        Dk = list(range(4 * qb, 4 * qb + 4))
                        oth = [k_ for k_ in range(NKC) if k_ not in Dk]
                        order = []
                        oi = 0
                        for d_ in Dk:
                            order += oth[oi:oi + 3]
                            oi += 3
                            order.append(d_)
                        order += oth[oi:]
                        assert sorted(order) == list(range(NKC))
                        prev = qk(order[0], 0)
                        for pos in range(NKC):
                            kc = order[pos]
                            nxt = qk(order[pos + 1], pos + 1) if pos + 1 < NKC else None
                            rest(kc, prev, pti, pos)
                            if pos == min(10, NKC - 1) and pending[0] is not None:
                                pending[0](2 * (pos % 2))
                                pending[0] = None
                            if pos == min(2, NKC - 1) and qb + 1 < NTT:
                                qproj(qb + 1, None)
                            pti += 2
                            prev = nxt
                        r0, o0, r1, t1 = fA[0][:, 512:1024], fA[1][:, 512:1024], fA[2][:, 0:512], fA[3][:, 0:512]
                        CP("act", r0, ps[bz[0]][:], [r_ps[bz[0]]], [r_fA[0]])
                        CP("dve", o0, ps[bo[0]][:], [r_ps[bo[0]]], [r_fA[1]])
                        CP("act", r1, ps[bz[1]][:], [r_ps[bz[1]]], [r_fA[2]])
                        CP("dve", t1, ps[bo[1]][:], [r_ps[bo[1]]], [r_fA[3]])
                        P.emit("dve", lambda e, r0=r0: e.reciprocal(r0, r0), [r_fA[0]], [r_fA[0]])
                        TT("dve", o0, o0, r0, ALU.mult, [r_fA[0], r_fA[1]], [r_fA[1]])
                        P.emit("dve", lambda e, r1=r1: e.reciprocal(r1, r1), [r_fA[2]], [r_fA[2]])
                        TT("dve", t1, t1, r1, ALU.mult, [r_fA[2], r_fA[3]], [r_fA[3]])
                        STT("dve", o0, t1, neglam[:, l:l + 1], o0, ALU.mult, ALU.add, [r_fA[3], r_fA[1], r_c], [r_fA[1]])
                        TT("dve", sqb, o0, o0, ALU.mult, [r_fA[1]], [r_sq])
                        def tail(b, h=h, l=l, Q0=Q0, o0=o0, r1=r1):
                            MM(ps[b][:], ones_bf[:], sqb, True, True, [r_c, r_sq], [r_ps[b]])
                            TS("dve", r1, ps[b][:], 1.0 / 128.0, EPS, ALU.mult, ALU.add, [r_ps[b]], [r_fA[2]])
                            ACTV(r1, r1, AF.Ln, [r_fA[2]], [r_fA[2]])
                            ACTV(r1, r1, AF.Exp, [r_fA[2]], [r_fA[2]], scale=-0.5)
                            STT("dve", ob, o0, gsub[:, l:l + 1], r1, ALU.mult, ALU.mult, [r_fA[1], r_fA[2], r_c], [r_ob])
                            DMA("pool", mix_b[:, 2 + h, Q0:Q0 + 512], ob, [r_ob], [R()])

                        pending[0] = tail
                if pending[0] is not None:
                    pending[0](0)
                    pending[0] = None
                P.barrier()
                MARKS.append(('B', sum(1 for o_ in P.ops['pe'] if o_.fn is not None)))
                _chk('B')

                wo = reg[:, 0:8192].rearrange("p (f n) -> p f n", n=1024)
                mt = [reg[:, 8192 + i * 4096: 8192 + (i + 1) * 4096].rearrange("p (f t) -> p f t", t=512) for i in range(2)]
                r_wo, r_mt = R(), [R(), R()]
                DMA("sp", wo, wo_b[l], [], [r_wo])
                DMA("sp", lnp[:], lnp_d[l, 0], [], [r_lnp])
                pendT = []
                for tt in range(NTT):
                    T0 = tt * 512
                    m_ = mt[tt % 2]
                    DMA("sp", m_, mix_b[:, :, T0:T0 + 512], [], [r_mt[tt % 2]])
                    for sub in range(4):
                        t0 = T0 + sub * 128
                        sl = sub % 2
                        DMA("sp", xres[sl][:], xin_d[tok0 + t0: tok0 + t0 + 128, :], [], [r_xres[sl]])
                        yb = [nextbank(list(range(8))), nextbank(list(range(8)))]
                        for hf in range(2):
                            for fc in range(8):
                                MM(ps[yb[hf]][:], m_[:, fc, sub * 128:(sub + 1) * 128], wo[:, fc, hf * 512:(hf + 1) * 512],
                                   fc == 0, fc == 7, [r_mt[tt % 2], r_wo], [r_ps[yb[hf]]])
                        s3 = (tt * 4 + sub) % 3
                        pendT.append(ln_epilogue(yb, xres[sl], r_xres[sl], xa_d[tok0 + t0: tok0 + t0 + 128, :], t0, True, s3, 3 + s3))
                        if len(pendT) > 2:
                            pendT.pop(0)()
                while pendT:
                    pendT.pop(0)()
                P.barrier()
                MARKS.append(('O', sum(1 for o_ in P.ops['pe'] if o_.fn is not None)))
                _chk('O')

                hT = reg[:, 0:NFC * 512].rearrange("p (f t) -> p f t", t=512)
                wd = reg[:, NFC * 512: NFC * 512 + NFC * 1024].rearrange("p (f n) -> p f n", n=1024)
                assert NFC * 512 + NFC * 1024 <= REG
                r_hT, r_wd = R(), R()
                DMA("sp", wd, wd_b[l], [], [r_wd])
                DMA("sp", lnp[:], lnp_d[l, 1], [], [r_lnp])
                gi = 0
                pendF = []
                for tt in range(NTT):
                    T0 = tt * 512
                    for fg in range(NFC // 2):
                        ws = wring[gi % 3]
                        rws = r_wring[gi % 3]
                        gi += 1
                        DMA("sp", ws[:, :, 0:256], wg_b[l][:, :, fg * 256:(fg + 1) * 256], [], [rws[0]])
                        DMA("sp", ws[:, :, 256:512], wu_b[l][:, :, fg * 256:(fg + 1) * 256], [], [rws[1]])
                        if fg == 1 and pendF:
                            pendF.pop(0)()
                        for f2 in range(2):
                            fc = fg * 2 + f2
                            bg = nextbank([0, 1, 2, 3])
                            for dc in range(DC):
                                MM(ps[bg][:], ws[:, dc, f2 * 128:(f2 + 1) * 128], xT[:, dc, T0:T0 + 512], dc == 0, dc == DC - 1, [rws[0], r_xT[tt]], [r_ps[bg]])
                            bu = nextbank([0, 1, 2, 3])
                            for dc in range(DC):
                                MM(ps[bu][:], ws[:, dc, 256 + f2 * 128: 256 + (f2 + 1) * 128], xT[:, dc, T0:T0 + 512], dc == 0, dc == DC - 1, [rws[1], r_xT[tt]], [r_ps[bu]])
                            k_ = fc % 2
                            th = fA[4 + k_][:, 0:512]
                            s_ = fA[4 + k_][:, 512:1024]
                            ACTV(th, ps[bg][:], AF.Tanh, [r_ps[bg]], [r_fA[4 + k_]], scale=0.5)
                            STT("dve", s_, th, 1.0, ps[bg][:], ALU.add, ALU.mult, [r_fA[4 + k_], r_ps[bg]], [r_fA[4 + k_]])
                            STT("dve", hT[:, fc, :], s_, 0.5, ps[bu][:], ALU.mult, ALU.mult, [r_fA[4 + k_], r_ps[bu]], [r_hT])
                    for sub in range(4):
                        t0 = T0 + sub * 128
                        sl = sub % 2
                        DMA("sp", xres[sl][:], xa_d[tok0 + t0: tok0 + t0 + 128, :], [], [r_xres[sl]])
                        yb = [nextbank([4, 5, 6, 7]), nextbank([4, 5, 6, 7])]
                        for hf in range(2):
                            for fc in range(NFC):
                                MM(ps[yb[hf]][:], hT[:, fc, sub * 128:(sub + 1) * 128], wd[:, fc, hf * 512:(hf + 1) * 512],
                                   fc == 0, fc == NFC - 1, [r_hT, r_wd], [r_ps[yb[hf]]])
                        cl = ln_epilogue(yb, xres[sl], r_xres[sl], xout_d[tok0 + t0: tok0 + t0 + 128, :], t0, not last, sl, 2 + sl)
                        if pendF:
                            pendF.pop(0)()
                        if cl is not None:
                            pendF.append(cl)
                while pendF:
                    pendF.pop(0)()
                P.barrier()
                MARKS.append(('F', sum(1 for o_ in P.ops['pe'] if o_.fn is not None)))
            tok0 += S

    except _Stop:
        pass

    P.barrier()
    P.finalize()
    with ExitStack() as es2:
        engsem = {e: es2.enter_context(nc.semaphore(f"sem_{e}")) for e in Prog.ENGS}
        dmasem = {e: [es2.enter_context(nc.semaphore(f"dsem_{e}{i}")) for i in range(Prog.NSLOT)] for e in ("sp", "pool")}
        block = es2.enter_context(nc.Block())

        @block.tensor
        def _(e):
            P.replay("pe", e, engsem, dmasem)

        @block.scalar
        def _(e):
            P.replay("act", e, engsem, dmasem)

        @block.vector
        def _(e):
            P.replay("dve", e, engsem, dmasem)

        @block.gpsimd
        def _(e):
            P.replay("pool", e, engsem, dmasem)

        @block.sync
        def _(e):
            P.replay("sp", e, engsem, dmasem)
    es.close()
    return nc


def prep_shared(inp, seq_lens):
    f = np.float32
    plans, tiles = na_tiles(seq_lens)
    nt = tiles.shape[0]
    sh = {}
    for k in ("w_in", "w_out", "w_gate", "w_up", "w_down"):
        sh[k] = np.ascontiguousarray(inp[k], dtype=f)
    g = np.asarray(inp["a_ln_g"], f)
    b = np.asarray(inp["a_ln_b"], f)
    algb = np.stack([np.tile(g, (1, 2)), np.tile(b, (1, 2))], axis=1)
    sh["algb"] = np.ascontiguousarray(np.broadcast_to(algb[:, None], (DEPTH, 128, 2, 512)))
    ws = np.asarray(inp["a_w_s"], f)
    sh["wsT"] = np.ascontiguousarray(ws.transpose(0, 3, 1, 2))
    sh["bs"] = np.ascontiguousarray(np.asarray(inp["a_b_s"], f).reshape(DEPTH, 1, 512))
    lam = np.stack([inp["b_lambda_q1"], inp["b_lambda_k1"], inp["b_lambda_q2"], inp["b_lambda_k2"]], axis=1).astype(f)
    sh["lamp"] = np.ascontiguousarray(np.broadcast_to(lam[:, None], (DEPTH, 128, 4, 64)))
    sh["subg"] = np.ascontiguousarray(np.asarray(inp["b_subln_g"], f).reshape(DEPTH, 128, 1))
    rpb = np.asarray(inp["c_rpb"], f).reshape(DEPTH, 4, 15 * 31)
    cb = np.full((DEPTH, 128, nt, 4, 128), NEG, f)
    msk = tiles >= 0
    idx = np.where(msk, tiles, 0)
    for l in range(DEPTH):
        for h in range(4):
            v = rpb[l, h][idx]
            cb[l, :, :, h, :] = np.where(msk, v, f(NEG)).transpose(1, 0, 2)
    sh["cb"] = cb
    lg = np.asarray(inp["ln_g"], f)
    lb = np.asarray(inp["ln_b"], f)
    lnp = np.stack([lg, lb], axis=2)
    sh["lnp"] = np.ascontiguousarray(np.broadcast_to(lnp[:, :, None], (DEPTH, 2, 128, 2, D)))
    bcol, qaug, dtab = host_consts()
    sh["bcol"], sh["qaug"], sh["dtab"] = bcol, qaug, dtab
    sh["ident"] = np.eye(128, dtype=f)
    return sh, nt


_CACHE = {}


def kernel(**inp):
    seq_lens = [2048, 4096, 4096]
    n = 8
    sh, nt = prep_shared(inp, seq_lens)
    xp = np.asarray(inp["x_prompt"], np.float32)
    xs = np.asarray(inp["x_sample"], np.float32)
    key = (tuple(seq_lens), nt)
    if key not in _CACHE:
        _CACHE[key] = build(seq_lens, nt)
    nc = _CACHE[key]
    in_maps = []
    for c in range(n):
        xc = np.concatenate([xp[c], xs[2 * c], xs[2 * c + 1]], axis=0)
        m = dict(sh)
        m["x"] = np.ascontiguousarray(xc)
        in_maps.append(m)
    res = run_bass_kernel_spmd(nc, in_maps, core_ids=list(range(n)))
    yp = np.empty_like(xp)
    ys = np.empty_like(xs)
    for c in range(n):
        y = np.asarray(res.results[c]["y"], np.float32)
        yp[c] = y[0:2048]
        ys[2 * c] = y[2048:6144]
        ys[2 * c + 1] = y[6144:10240]
    return (yp, ys)
```

```python
import math
import numpy as np
import concourse.bass as bass
import concourse.mybir as mybir
from concourse.bass_utils import run_bass_kernel_spmd

F32 = mybir.dt.float32
BF16 = mybir.dt.bfloat16
ALU = mybir.AluOpType
AF = mybir.ActivationFunctionType
AX = mybir.AxisListType

D = 1024
DC = 8
INW = 2816
DFF = 2816
NFC = 22
DEPTH = 2
ALPHA = (2 * DEPTH) ** 0.25
EPS = 1e-5
SLOPES = [2.0 ** (-2.0 * (h + 1)) for h in range(4)]
LAM_INIT = [0.8 - 0.6 * math.exp(-0.3 * l) for l in range(DEPTH)]
NEG = -30000.0
OFF_AU, OFF_AV, OFF_BQ, OFF_BK, OFF_BV, OFF_CQ, OFF_CK, OFF_CV = 0, 256, 512, 1024, 1536, 2048, 2304, 2560


class Op:
    __slots__ = ("eng", "fn", "deps", "sig", "sigidx", "is_dma", "slot", "dval")


class Res:
    __slots__ = ("w", "r")

    def __init__(self):
        self.w = None
        self.r = {}


class Prog:
    ENGS = ["pe", "act", "dve", "pool", "sp"]
    NSLOT = 8

    def __init__(self):
        self.ops = {e: [] for e in self.ENGS}
        self.last = {e: None for e in self.ENGS}
        self.slots = {e: [None] * self.NSLOT for e in self.ENGS}
        self.rr = {e: 0 for e in self.ENGS}
        self.cnt = {}
        self.out_dmas = []

    def emit(self, eng, fn, reads=(), writes=(), dma=False, extra=()):
        op = Op()
        op.eng, op.fn, op.sig, op.sigidx, op.is_dma, op.slot, op.dval = eng, fn, False, 0, dma, 0, 0
        deps = set(extra)
        for r in reads:
            if r.w is not None:
                deps.add(r.w)
        for w in writes:
            if w.w is not None:
                deps.add(w.w)
            for o in w.r.values():
                deps.add(o)
        if dma:
            i = self.rr[eng]
            self.rr[eng] = (i + 1) % self.NSLOT
            prev = self.slots[eng][i]
            if prev is not None:
                deps.add(prev)
            self.slots[eng][i] = op
            c = self.cnt.get((eng, i), 0) + 1
            self.cnt[(eng, i)] = c
            op.slot, op.dval = i, 16 * c
        fin = []
        for d in deps:
            if d is op:
                continue
            if (not d.is_dma) and d.eng == eng and eng == "pe":
                continue
            if not d.is_dma:
                d.sig = True
            fin.append(d)
        op.deps = fin
        for r in reads:
            r.r[id(op) if dma else eng] = op
        for w in writes:
            w.w = op
            w.r = {}
        self.ops[eng].append(op)
        if fn is not None and not dma:
            self.last[eng] = op
        return op

    def barrier(self):
        lasts = [o for o in self.last.values() if o is not None]
        dm = [o for e in self.ENGS for o in self.slots[e] if o is not None]
        for e in self.ENGS:
            self.emit(e, None, extra=[o for o in lasts + dm])

    def finalize(self):
        for e in self.ENGS:
            n = 0
            for op in self.ops[e]:
                if op.sig and not op.is_dma:
                    n += 1
                    op.sigidx = n

    def replay(self, eng, e, engsem, dmasem):
        seen = {}
        for op in self.ops[eng]:
            for d in op.deps:
                if d.is_dma:
                    key, sem, val = ("d", d.eng, d.slot), dmasem[d.eng][d.slot], d.dval
                else:
                    key, sem, val = ("e", d.eng), engsem[d.eng], d.sigidx
                if seen.get(key, 0) < val:
                    e.wait_ge(sem, val)
                    seen[key] = val
            if op.fn is not None:
                inst = op.fn(e)
                if op.is_dma:
                    inst.then_inc(dmasem[eng][op.slot], 16)
                elif op.sig:
                    inst.then_inc(engsem[eng], 1)


def na_pattern(S):
    rows = S // 64
    wr = min(8, rows)
    res = []
    ql = np.arange(128)
    for i in range(S // 128):
        r = 2 * i + ql // 64
        c = ql % 64
        rs = np.clip(r - wr // 2, 0, rows - wr)
        cs = np.clip(c - 8, 0, 48)
        t_lo = 64 * rs.min()
        t_hi = 64 * (rs.max() + wr)
        lst = []
        for kc in range(t_lo // 128, (t_hi + 127) // 128):
            k = kc * 128 + np.arange(128)
            kr = (k // 64)[:, None]
            kcol = (k % 64)[:, None]
            valid = (kr >= rs[None]) & (kr < rs[None] + wr) & (kcol >= cs[None]) & (kcol < cs[None] + 16)
            idx = (kr - r[None] + 7) * 31 + (kcol - c[None] + 15)
            idx = np.where(valid, idx, -1).astype(np.int32)
            lst.append((kc, idx))
        res.append(lst)
    return res


def na_tiles(seq_lens):
    uniq = {}
    tiles = []
    plans = {}
    for S in sorted(set(seq_lens)):
        plan = []
        for lst in na_pattern(S):
            pl = []
            for kc, idx in lst:
                key = idx.tobytes()
                if key not in uniq:
                    uniq[key] = len(tiles)
                    tiles.append(idx)
                pl.append((kc, uniq[key]))
            plan.append(pl)
        plans[S] = plan
    return plans, np.stack(tiles)


def host_consts():
    p = np.arange(128, dtype=np.float64)
    bcol = np.zeros((128, 4, 64), np.float32)
    qaug = np.zeros((4, 2, 2, 512), np.float32)
    dtab = np.zeros((4, 128, 4, 512), np.float32)
    ql = np.arange(512)
    for h in range(4):
        s = SLOPES[h]
        for m in range(1, 32):
            bcol[:, h, m] = s * (p - 128 * m)
        for m in range(4, 32):
            bcol[:, h, 32 + m] = -s * (p + 128 * m - 511)
        qaug[h, 0, 0] = -s * (ql & 0xFF)
        qaug[h, 0, 1] = -s * (ql & 0x100)
        b = 511 - ql
        qaug[h, 1, 0] = -s * (b & 0xFF)
        qaug[h, 1, 1] = -s * (b & 0x100)
        for j in range(4):
            dtab[h, :, j, :] = np.exp(-s * np.abs(ql[None, :] - 128 * j - p[:, None]))
    return bcol, qaug, dtab


class _Stop(Exception):
    pass


KSTOP = [None]


MARKS = []


def _chk(tag):
    if KSTOP[0] == tag:
        raise _Stop()


def build(seq_lens, n_tiles):
    NT = sum(seq_lens)
    plans, _ = na_tiles(seq_lens)
    nc = bass.Bass("TRN2", target_bir_lowering=False)
    P = Prog()

    def din(name, shape):
        return nc.dram_tensor(name, list(shape), F32, kind="ExternalInput").ap()

    x_d = din("x", [NT, D])
    w_in_d = din("w_in", [DEPTH, D, INW])
    w_out_d = din("w_out", [DEPTH, D, D])
    w_gate_d = din("w_gate", [DEPTH, D, DFF])
    w_up_d = din("w_up", [DEPTH, D, DFF])
    w_down_d = din("w_down", [DEPTH, DFF, D])
    algb_d = din("algb", [DEPTH, 128, 2, 512])
    wsT_d = din("wsT", [DEPTH, 128, 4, 128])
    bs_d = din("bs", [DEPTH, 1, 512])
    lamp_d = din("lamp", [DEPTH, 128, 4, 64])
    subg_d = din("subg", [DEPTH, 128, 1])
    cb_d = din("cb", [DEPTH, 128, n_tiles, 4, 128])
    lnp_d = din("lnp", [DEPTH, 2, 128, 2, D])
    bcol_d = din("bcol", [128, 4, 64])
    qaug_d = din("qaug", [4, 2, 2, 512])
    dtab_d = din("dtab", [4, 128, 4, 512])
    ident_d = din("ident", [128, 128])
    y_d = nc.dram_tensor("y", [NT, D], F32, kind="ExternalOutput").ap()

    def dscr(name, shape, dt):
        return nc.dram_tensor(name, list(shape), dt, kind="Internal").ap()

    xa_d = dscr("xa_s", [NT, D], F32)
    xb_d = dscr("xb_s", [NT, D], F32)
    win_b = dscr("win_b", [DEPTH, 128, DC, INW], BF16)
    wg_b = dscr("wg_b", [DEPTH, 128, DC, DFF], BF16)
    wu_b = dscr("wu_b", [DEPTH, 128, DC, DFF], BF16)
    wo_b = dscr("wo_b", [DEPTH, 128, DC, D], BF16)
    wd_b = dscr("wd_b", [DEPTH, 128, NFC, D], BF16)
    cb_b = dscr("cb_b", [DEPTH, 128, n_tiles * 512], BF16)
    qaug_b = dscr("qaug_b", [4, 2, 2, 512], BF16)
    SMAX = max(seq_lens)
    mix_b = dscr("mix_b", [128, 8, SMAX], BF16)

    from contextlib import ExitStack
    es = ExitStack()

    def sb(name, shape, dt):
        return es.enter_context(nc.sbuf_tensor("s_" + name, list(shape), dt))

    REG = 34816
    xT = sb("xT", [128, DC, SMAX], BF16)
    reg = sb("reg", [128, REG], BF16)
    wring = [sb(f"wring{i}", [128, DC, 512], BF16) for i in range(3)]
    ident_f = sb("ident_f", [128, 128], F32)
    ident_bf = sb("ident_bf", [128, 128], BF16)
    ones_bf = sb("ones_bf", [128, 128], BF16)
    ones64 = sb("ones64", [64, 128], BF16)
    mhalf = sb("mhalf", [128, 512], F32)
    bcol = sb("bcol", [128, 4, 64], F32)
    wsT = sb("wsT_sb", [128, DEPTH, 4, 128], BF16)
    bsr = sb("bsr", [64, DEPTH, 512], BF16)
    neglam = sb("neglam", [128, DEPTH], F32)
    gsub = sb("gsub", [128, DEPTH], F32)
    fA = [sb(f"fA{i}", [128, 1024], F32) for i in range(6)]
    xres = [sb(f"xres{i}", [128, 1024], F32) for i in range(2)]
    lnp = sb("lnp_sb", [128, 2, D], F32)
    st6 = sb("st6", [128, 4, 6], F32)
    st2 = sb("st2", [128, 8], F32)
    xbf = reg[:, 8192:9216]
    ps = [es.enter_context(nc.psum_tensor(f"ps{i}", [128, 512], F32)) for i in range(8)]

    R = lambda: Res()
    r_xT = [R() for _ in range(SMAX // 512)]
    r_reg = R()
    r_wring = [[R(), R(), R()] for _ in range(3)]
    r_ps = [R() for _ in range(8)]
    r_fA = [R() for _ in range(6)]
    r_xres = [R(), R()]
    r_c = R()
    r_lnp, r_st, r_xbf = R(), R(), R()
    r_dram = R()

    def MM(out, lhsT, rhs, start, stop, reads, writes):
        return P.emit("pe", lambda e: e.matmul(out, lhsT, rhs, start=start, stop=stop, skip_group_check=True), reads, writes)

    def ACTV(out, in_, func, reads, writes, bias=0.0, scale=1.0):
        return P.emit("act", lambda e: e.activation(out, in_, func, bias=bias, scale=scale), reads, writes)

    def TS(eng, out, in0, s1, s2, op0, op1, reads, writes):
        return P.emit(eng, lambda e: e.tensor_scalar(out, in0, s1, s2, op0, op1), reads, writes)

    def TT(eng, out, in0, in1, op, reads, writes):
        return P.emit(eng, lambda e: e.tensor_tensor(out, in0, in1, op), reads, writes)

    def STT(eng, out, in0, sc, in1, op0, op1, reads, writes):
        return P.emit(eng, lambda e: e.scalar_tensor_tensor(out, in0, sc, in1, op0, op1), reads, writes)

    def CP(eng, out, in_, reads, writes):
        if eng == "act":
            return P.emit("act", lambda e: e.copy(out, in_), reads, writes)
        return P.emit(eng, lambda e: e.tensor_copy(out, in_), reads, writes)

    def DMA(q, out, in_, reads, writes):
        return P.emit(q, lambda e: e.dma_start(out=out, in_=in_), reads, writes, dma=True)

    def MEMSET(eng, ap, val, writes):
        return P.emit(eng, lambda e: e.memset(ap, val), (), writes)

    bank_rr = [0]

    def nextbank(pool):
        b = pool[bank_rr[0] % len(pool)]
        bank_rr[0] += 1
        return b

    try:
        stg_f = [fA[0], fA[1], fA[2], fA[3], fA[4], fA[5], xres[0], xres[1]]
        stg_r = r_fA + r_xres
        def conv(src, dst, n_inner, k):
            a, b = src.shape[1], src.shape[2]
            per = max(1, 1024 // b)
            i = 0
            while i < a:
                na = min(per, a - i)
                slot = k[0] % 8
                k[0] += 1
                f = stg_f[slot][:, 0:na * b].rearrange("p (a b) -> p a b", b=b)
                g = reg[:, slot * 1024: slot * 1024 + na * b].rearrange("p (a b) -> p a b", b=b)
                DMA("sp", f, src[:, i:i + na, :], [], [stg_r[slot]])
                eng = ["dve", "act"][k[0] % 2]
                CP(eng, g, f, [stg_r[slot]], [r_st_slots[slot]])
                DMA("pool", dst[:, i:i + na, :], g, [r_st_slots[slot]], [R()])
                i += na

        r_st_slots = [R() for _ in range(8)]
        kk = [0]
        for l in range(DEPTH):
            for (srcw, dstw, ncol) in ((w_in_d, win_b, INW), (w_gate_d, wg_b, DFF), (w_up_d, wu_b, DFF), (w_out_d, wo_b, D)):
                s3 = srcw[l].rearrange("(dc p) n -> p dc n", p=128)
                for c0 in range(0, ncol, 512):
                    c1 = min(ncol, c0 + 512)
                    conv(s3[:, :, c0:c1], dstw[l][:, :, c0:c1], c1 - c0, kk)
            s3 = w_down_d[l].rearrange("(fc p) n -> p fc n", p=128)
            conv(s3, wd_b[l], D, kk)
            cbs = cb_d[l].rearrange("p t h q -> p t (h q)")
            conv(cbs, cb_b[l].rearrange("p (t x) -> p t x", x=512), 512, kk)
        DMA("sp", ident_f[:], ident_d, [], [r_c])
        CP("dve", ident_bf[:], ident_f[:], [r_c], [r_c])
        MEMSET("dve", ones_bf[:], 1.0, [r_c])
        MEMSET("dve", ones64[:], 0.0, [r_c])
        MEMSET("dve", ones64[0:1, :], 1.0, [r_c])
        MEMSET("dve", ones64[32:33, :], 1.0, [r_c])
        MEMSET("pool", mhalf[:], -0.5, [r_c])
        DMA("sp", bcol[:], bcol_d, [], [r_c])
        qa_f = fA[4][0:16, 0:512]
        DMA("sp", qa_f, qaug_d.rearrange("h a r q -> (h a r) q"), [], [r_fA[4]])
        qa_b = xbf[0:16, 0:512]
        CP("dve", qa_b, qa_f, [r_fA[4]], [r_xbf])
        DMA("pool", qaug_b.rearrange("h a r q -> (h a r) q"), qa_b, [r_xbf], [R()])
        for l in range(DEPTH):
            t = fA[5][:, 0:512].rearrange("p (g t) -> p g t", t=128)
            DMA("sp", t, wsT_d[l], [], [r_fA[5]])
            CP("dve", wsT[:, l, :, :], t, [r_fA[5]], [r_c])
            bt = fA[4][0:64, 0:512]
            MEMSET("dve", bt, 0.0, [r_fA[4]])
            DMA("sp", fA[4][0:1, 0:512], bs_d[l], [], [r_fA[4]])
            DMA("sp", fA[4][32:33, 0:512], bs_d[l], [], [r_fA[4]])
            hi = xbf[0:64, 0:512]
            lo = xbf[0:64, 512:1024]
            CP("dve", hi, bt, [r_fA[4]], [r_xbf])
            tmp = fA[3][0:64, 0:512]
            TT("dve", tmp, bt, hi, ALU.subtract, [r_fA[4], r_xbf], [r_fA[3]])
            CP("dve", lo, tmp, [r_fA[3]], [r_xbf])
            MEMSET("dve", bsr[:, l, :], 0.0, [r_c])
            CP("dve", bsr[0:1, l, :], xbf[0:1, 0:512], [r_xbf], [r_c])
            CP("dve", bsr[32:33, l, :], xbf[32:33, 512:1024], [r_xbf], [r_c])
            lp = fA[5][:, 0:256].rearrange("p (a b) -> p a b", b=64)
            DMA("sp", lp, lamp_d[l], [], [r_fA[5]])
            pr = fA[3][:, 0:128].rearrange("p (a b) -> p a b", b=64)
            TT("dve", pr[:, 0, :], lp[:, 0, :], lp[:, 1, :], ALU.mult, [r_fA[5]], [r_fA[3]])
            TT("dve", pr[:, 1, :], lp[:, 2, :], lp[:, 3, :], ALU.mult, [r_fA[5]], [r_fA[3]])
            P.emit("dve", lambda e, pr=pr: e.reduce_sum(st2[:, 0:1], pr[:, 0, :], AX.X), [r_fA[3]], [r_st])
            P.emit("dve", lambda e, pr=pr: e.reduce_sum(st2[:, 1:2], pr[:, 1, :], AX.X), [r_fA[3]], [r_st])
            ACTV(st2[:, 2:4], st2[:, 0:2], AF.Exp, [r_st], [r_st])
            TT("dve", st2[:, 4:5], st2[:, 3:4], st2[:, 2:3], ALU.subtract, [r_st], [r_st])
            TS("dve", neglam[:, l:l + 1], st2[:, 4:5], -LAM_INIT[l], None, ALU.add, ALU.bypass, [r_st], [r_c])
            DMA("sp", st2[:, 5:6], subg_d[l], [], [r_st])
            TS("dve", gsub[:, l:l + 1], st2[:, 5:6], 1.0 - LAM_INIT[l], None, ALU.mult, ALU.bypass, [r_st], [r_c])
        P.barrier()
        MARKS.append(('prologue', sum(1 for o_ in P.ops['pe'] if o_.fn is not None)))
        _chk('prologue')

        def transpose_to_xT(src_f32, src_res, t0):
            for half in range(2):
                b = nextbank(list(range(8)))
                for i in range(4):
                    dc = half * 4 + i
                    P.emit("pe", lambda e, b=b, i=i, dc=dc: e.transpose(ps[b][:, i * 128:(i + 1) * 128], src_f32[:, dc * 128:(dc + 1) * 128], ident_f[:]),
                           [src_res, r_c], [r_ps[b]])
                CP("act", xT[:, half * 4:half * 4 + 4, t0:t0 + 128],
                   ps[b][:].rearrange("p (a t) -> p a t", t=128), [r_ps[b]], [r_xT[t0 // 512]])

        def ln_epilogue(ybanks, xr, xr_res, out_dram_rows, t0, do_T, zi, oi):
            z, zr = fA[zi], r_fA[zi]
            o, orr = fA[oi], r_fA[oi]
            for hf in range(2):
                STT("dve", z[:, hf * 512:(hf + 1) * 512], xr[:, hf * 512:(hf + 1) * 512], ALPHA, ps[ybanks[hf]][:],
                    ALU.mult, ALU.add, [xr_res, r_ps[ybanks[hf]]], [zr])
            for hf in range(2):
                P.emit("dve", lambda e, hf=hf: e.bn_stats(st6[:, hf, :], z[:, hf * 512:(hf + 1) * 512]), [zr], [r_st])
            P.emit("dve", lambda e: e.bn_aggr(st2[:, 0:2], st6[:, 0:2, :].rearrange("p a b -> p (a b)")), [r_st], [r_st])
            TS("dve", st2[:, 2:3], st2[:, 1:2], EPS, None, ALU.add, ALU.bypass, [r_st], [r_st])
            TT("pool", st2[:, 3:4], st2[:, 2:3], mhalf[:, 0:1], ALU.pow, [r_st, r_c], [r_st])
            STT("dve", st2[:, 4:5], st2[:, 0:1], -1.0, st2[:, 3:4], ALU.mult, ALU.mult, [r_st], [r_st])
            ACTV(z[:], z[:], AF.Identity, [zr, r_st], [zr], bias=st2[:, 4:5], scale=st2[:, 3:4])
            TT("dve", z[:], z[:], lnp[:, 0, :], ALU.mult, [zr, r_lnp], [zr])
            TT("dve", o[:], z[:], lnp[:, 1, :], ALU.add, [zr, r_lnp], [orr])
            DMA("pool", out_dram_rows, o[:], [orr], [R()])
            if do_T:
                return lambda: transpose_to_xT(o, orr, t0)
            return None

        tok0 = 0
        for si, S in enumerate(seq_lens):
            NTT = S // 512
            NKC = S // 128
            for t in range(NKC):
                xb4 = [xres[0], xres[1], fA[0], fA[1]][t % 4]
                rb4 = [r_xres[0], r_xres[1], r_fA[0], r_fA[1]][t % 4]
                DMA("sp", xb4[:], x_d[tok0 + t * 128: tok0 + (t + 1) * 128, :], [], [rb4])
                transpose_to_xT(xb4, rb4, t * 128)
            P.barrier()
            MARKS.append(('x0', sum(1 for o_ in P.ops['pe'] if o_.fn is not None)))
            _chk('x0')

            for l in range(DEPTH):
                xin_d = x_d if l == 0 else xb_d
                last = (l == DEPTH - 1)
                xout_d = y_d if last else xb_d
                wA = wring[0]
                DMA("sp", wA[:], win_b[l][:, :, 0:512], [], [r_wring[0][0]])
                algb = fA[5][:].rearrange("p (a b) -> p a b", b=512)
                r_algb = r_fA[5]
                DMA("sp", algb, algb_d[l], [], [r_algb])
                uT = reg[:, 0:1024].rearrange("p (j t) -> p j t", t=512)
                vln = reg[:, 1024:2048].rearrange("p (s c) -> p s c", c=256)
                oa = reg[:, 2048:3072].rearrange("p (j t) -> p j t", t=512)
                r_uT, r_vln, r_oa = R(), R(), R()

                def gelu_chain(src_ps, bank, width, dst, dst_res, fa, fb):
                    xs = fA[fa][:, 0:width]
                    tt_ = fA[fb][:, 0:width]
                    ACTV(xs, src_ps, AF.Identity, [r_ps[bank]], [r_fA[fa]], scale=0.5)
                    ACTV(tt_, src_ps, AF.Square, [r_ps[bank]], [r_fA[fb]], scale=0.5)
                    STT("dve", tt_, tt_, 1.0 / 0.17886, xs, ALU.add, ALU.mult, [r_fA[fb], r_fA[fa]], [r_fA[fb]])
                    ACTV(tt_, tt_, AF.Tanh, [r_fA[fb]], [r_fA[fb]], scale=1.5957691216 * 0.17886)
                    STT("dve", dst, tt_, 1.0, xs, ALU.add, ALU.mult, [r_fA[fb], r_fA[fa]], [dst_res])

                for tt in range(NTT):
                    T0 = tt * 512
                    for j in range(2):
                        b = nextbank(list(range(8)))
                        for dc in range(DC):
                            MM(ps[b][:], wA[:, dc, OFF_AU + j * 128: OFF_AU + (j + 1) * 128], xT[:, dc, T0:T0 + 512],
                               dc == 0, dc == DC - 1, [r_wring[0][0], r_xT[tt]], [r_ps[b]])
                        gelu_chain(ps[b][:], b, 512, uT[:, j, :], r_uT, j * 2, j * 2 + 1)
                    for sp2 in range(2):
                        b = nextbank(list(range(8)))
                        for s_ in range(2):
                            sub = sp2 * 2 + s_
                            for dc in range(DC):
                                MM(ps[b][:, s_ * 256:(s_ + 1) * 256], xT[:, dc, T0 + sub * 128: T0 + (sub + 1) * 128],
                                   wA[:, dc, OFF_AV:OFF_AV + 256], (dc == 0 and s_ == 0), dc == DC - 1,
                                   [r_wring[0][0], r_xT[tt]], [r_ps[b]])
                        vg = fA[4][:, 0:512]
                        gelu_chain(ps[b][:], b, 512, vg, r_fA[4], 0 + sp2 * 2, 1 + sp2 * 2)
                        for s_ in range(2):
                            P.emit("dve", lambda e, s_=s_, vg=vg: e.bn_stats(st6[:, s_, :], vg[:, s_ * 256:(s_ + 1) * 256]), [r_fA[4]], [r_st])
                            P.emit("dve", lambda e, s_=s_: e.bn_aggr(st2[:, 0:2], st6[:, s_, :]), [r_st], [r_st])
                            TS("dve", st2[:, 2:3], st2[:, 1:2], EPS, None, ALU.add, ALU.bypass, [r_st], [r_st])
                            TT("pool", st2[:, 3:4], st2[:, 2:3], mhalf[:, 0:1], ALU.pow, [r_st, r_c], [r_st])
                            TS("dve", vg[:, s_ * 256:(s_ + 1) * 256], vg[:, s_ * 256:(s_ + 1) * 256], st2[:, 0:1], st2[:, 3:4],
                               ALU.subtract, ALU.mult, [r_fA[4], r_st], [r_fA[4]])
                        TT("dve", vg, vg, algb[:, 0, :], ALU.mult, [r_fA[4], r_algb], [r_fA[4]])
                        TT("dve", vln[:, sp2 * 2:sp2 * 2 + 2, :], vg.rearrange("p (s c) -> p s c", c=256), algb[:, 1, :].rearrange("p (s c) -> p s c", c=256),
                           ALU.add, [r_fA[4], r_algb], [r_vln])
                    for sub in range(4):
                        b = nextbank(list(range(8)))
                        first = True
                        for ab in range(2):
                            for j in range(2):
                                g = 2 * j + ab
                                col = (ab * 2 + j) * 128
                                MM(ps[b][:, col:col + 128], vln[:, sub, j * 128:(j + 1) * 128], wsT[:, l, g, :], first, False,
                                   [r_vln, r_c], [r_ps[b]])
                                first = False
                                MM(ps[b][:, col:col + 128], ones64[:, :], bsr[:, l, g * 128:(g + 1) * 128], False, True,
                                   [r_c], [r_ps[b]])
                        for ab in range(2):
                            pa = slice(ab * 64, ab * 64 + 64)
                            TT("dve", oa[pa, :, sub * 128:(sub + 1) * 128],
                               ps[b][pa, ab * 256:(ab + 1) * 256].rearrange("p (j t) -> p j t", t=128),
                               uT[pa, :, sub * 128:(sub + 1) * 128], ALU.mult, [r_ps[b], r_uT], [r_oa])
                    DMA("pool", mix_b[:, 0:2, T0:T0 + 512], oa, [r_oa], [R()])
                P.barrier()
                MARKS.append(('A', sum(1 for o_ in P.ops['pe'] if o_.fn is not None)))
                _chk('A')

                cqT = reg[:, 0:2 * S].rearrange("p (j t) -> p j t", t=S)
                ckT = reg[:, 2 * S:4 * S].rearrange("p (j t) -> p j t", t=S)
                cv = reg[:, 4 * S:6 * S].rearrange("p (k c) -> p k c", c=256)
                cbt = reg[:, 6 * S:6 * S + n_tiles * 512].rearrange("p (t x) -> p t x", x=512)
                coff = 6 * S + n_tiles * 512
                PTc = [reg[:, coff + i * 512: coff + (i + 1) * 512] for i in range(3)]
                oc = reg[:, coff + 1536: coff + 2560].rearrange("p (j t) -> p j t", t=512)
                cqm = [reg[:, coff + 2560 + i * 512: coff + 3072 + i * 512].rearrange("p (h t) -> p h t", t=128) for i in range(2)]
                r_cqm = [R(), R()]
                assert coff + 3584 <= REG
                MEMSET("dve", reg[:, coff + 2560: coff + 3584], 0.0, [r_cqm[0], r_cqm[1]])
                r_cq, r_ck, r_cv, r_cbt, r_oc = [R() for _ in range(NTT)], [R() for _ in range(NTT)], [R() for _ in range(NTT)], R(), R()
                r_PTc = [R(), R(), R()]
                DMA("sp", wring[1][:], win_b[l][:, :, OFF_CQ:OFF_CQ + 512], [], [r_wring[1][0]])
                DMA("sp", wring[2][:, :, 0:256], win_b[l][:, :, OFF_CV:OFF_CV + 256], [], [r_wring[2][0]])
                DMA("sp", cbt, cb_b[l].rearrange("p (t x) -> p t x", x=512), [], [r_cbt])
                for tt in range(NTT):
                    T0 = tt * 512
                    for (dst, rr_, off, scl) in ((cqT, r_cq, 0, 0.125), (ckT, r_ck, 256, 1.0)):
                        for j in range(2):
                            b = nextbank(list(range(8)))
                            for dc in range(DC):
                                MM(ps[b][:], wring[1][:, dc, off + j * 128: off + (j + 1) * 128], xT[:, dc, T0:T0 + 512],
                                   dc == 0, dc == DC - 1, [r_wring[1][0], r_xT[tt]], [r_ps[b]])
                            if j == 0:
                                ACTV(dst[:, j, T0:T0 + 512], ps[b][:], AF.Identity, [r_ps[b]], [rr_[tt]], scale=scl)
                            else:
                                TS("dve", dst[:, j, T0:T0 + 512], ps[b][:], scl, None, ALU.mult, ALU.bypass, [r_ps[b]], [rr_[tt]])
                    for sp2 in range(2):
                        b = nextbank(list(range(8)))
                        for s_ in range(2):
                            sub = sp2 * 2 + s_
                            for dc in range(DC):
                                MM(ps[b][:, s_ * 256:(s_ + 1) * 256], xT[:, dc, T0 + sub * 128:T0 + (sub + 1) * 128],
                                   wring[2][:, dc, 0:256], (dc == 0 and s_ == 0), dc == DC - 1, [r_wring[2][0], r_xT[tt]], [r_ps[b]])
                        CP("act" if sp2 == 0 else "dve", cv[:, tt * 4 + sp2 * 2: tt * 4 + sp2 * 2 + 2, :],
                           ps[b][:].rearrange("p (s c) -> p s c", c=256), [r_ps[b]], [r_cv[tt]])
                plan = plans[S]
                r_cqm4 = [[R() for _ in range(4)] for _ in range(2)]

                def c_cqm(i):
                    for h in range(4):
                        j, bb = h // 2, h % 2
                        pa = slice(bb * 64, bb * 64 + 64)
                        CP("dve" if h % 2 == 0 else "act", cqm[i % 2][pa, h, :], cqT[pa, j, i * 128:(i + 1) * 128], [r_cq[i // 4]], [r_cqm4[i % 2][h]])

                def c_qk(i, n):
                    kc, tid = plan[i][n]
                    b = nextbank([0, 1, 2, 3])
                    MM(ps[b][:], ident_bf[:], cbt[:, tid, :], True, False, [r_c, r_cbt], [r_ps[b]])
                    for j in range(2):
                        MM(ps[b][:, j * 256:(j + 1) * 256], ckT[:, j, kc * 128:(kc + 1) * 128],
                           cqm[i % 2][:, 2 * j:2 * j + 2, :].rearrange("p h t -> p (h t)"),
                           False, True, [r_ck[kc // 4], r_cqm4[i % 2][2 * j], r_cqm4[i % 2][2 * j + 1]], [r_ps[b]])
                    return b

                pi = [0]

                def c_rest(i, n, b):
                    kc, tid = plan[i][n]
                    lst = plan[i]
                    b_out, b_z = 4 + 2 * (i % 2), 5 + 2 * (i % 2)
                    pt = pi[0] % 3
                    pi[0] += 1
                    ACTV(PTc[pt], ps[b][:], AF.Exp, [r_ps[b]], [r_PTc[pt]])
                    for j in range(2):
                        MM(ps[b_out][:, j * 256:(j + 1) * 256], cv[:, kc, j * 128:(j + 1) * 128], PTc[pt][:, j * 256:(j + 1) * 256],
                           (n == 0 and j == 0), n == len(lst) - 1, [r_cv[kc // 4], r_PTc[pt]], [r_ps[b_out]])
                    MM(ps[b_z][:], ones_bf[:], PTc[pt], n == 0, n == len(lst) - 1, [r_c, r_PTc[pt]], [r_ps[b_z]])
                    if n == len(lst) - 1:
                        tq = i // 4
                        rz = fA[i % 2][:, 0:512]
                        P.emit("dve", lambda e, rz=rz, b_z=b_z: e.reciprocal(rz, ps[b_z][:]), [r_ps[b_z]], [r_fA[i % 2]])
                        for h in range(4):
                            j, bb = h // 2, h % 2
                            pa = slice(bb * 64, bb * 64 + 64)
                            TT("dve", oc[pa, j, (i % 4) * 128:(i % 4 + 1) * 128], ps[b_out][pa, h * 128:(h + 1) * 128], rz[pa, h * 128:(h + 1) * 128],
                               ALU.mult, [r_ps[b_out], r_fA[i % 2]], [r_oc])
                        if i % 4 == 3:
                            DMA("pool", mix_b[:, 6:8, tq * 512:(tq + 1) * 512], oc, [r_oc], [R()])

                items = [(i, n) for i in range(NKC) for n in range(len(plan[i]))]
                c_cqm(0)
                if NKC > 1:
                    c_cqm(1)
                prevb = c_qk(*items[0])
                for ix, (i, n) in enumerate(items):
                    nxtb = c_qk(*items[ix + 1]) if ix + 1 < len(items) else None
                    c_rest(i, n, prevb)
                    prevb = nxtb
                    if n == len(plan[i]) - 1 and i + 2 < NKC:
                        c_cqm(i + 2)
                P.barrier()
                MARKS.append(('C', sum(1 for o_ in P.ops['pe'] if o_.fn is not None)))
                _chk('C')

                kTm = [reg[:, 0:S], reg[:, S:2 * S]]
                vB = reg[:, 2 * S:3 * S].rearrange("p (k e) -> p k e", e=128)
                qTa = reg[:, 3 * S:4 * S]
                o_ = 4 * S
                qv = [{}, {}]
                for par in range(2):
                    for ti, ty in enumerate(("L", "R", "N")):
                        for c in range(2):
                            qv[par][(ty, c)] = reg[:, o_:o_ + 512]
                            o_ += 512
                PT = []
                for i in range(6):
                    PT.append(reg[:, o_:o_ + 512])
                    o_ += 512
                ob = reg[:, o_:o_ + 512]
                o_ += 512
                sqb = reg[:, o_:o_ + 512]
                o_ += 512
                assert o_ <= REG
                r_k, r_v = [R() for _ in range(NTT)], [R() for _ in range(NTT)]
                r_PT, r_ob, r_sq, r_dt = [R() for _ in range(6)], R(), R(), R()
                r_qv = [{(ty, c): R() for ty in ("L", "R", "N") for c in range(2)} for _ in range(2)]
                r_qT = [R() for _ in range(NTT)]
                MEMSET("dve", reg[:, 0:2 * S], 0.0, [r_reg])
                MEMSET("dve", reg[64:66, 0:S], 1.0, [r_reg])
                MEMSET("dve", reg[0:2, S:2 * S], 1.0, [r_reg])
                MEMSET("dve", reg[:, 4 * S:4 * S + 12 * 512], 0.0, [r_reg])
                P.barrier()
                dt_sb = [fA[4], fA[5]]
                pending = [None]
                r_pf = [R(), R()]
                for h in range(4):
                    wB = wring[h % 2]
                    rwBs = r_wring[h % 2]
                    for n_, off in enumerate((OFF_BQ, OFF_BK, OFF_BV)):
                        DMA("sp", wB[:, :, n_ * 128:(n_ + 1) * 128], win_b[l][:, :, off + h * 128: off + (h + 1) * 128], [], [rwBs[n_]])
                    for jj in range(2):
                        DMA("sp", dt_sb[jj][:].rearrange("p (a b) -> p a b", b=512), dtab_d[h][:, jj * 2:jj * 2 + 2, :], [], [r_fA[4 + jj]])
                    for par in range(2):
                        for ai, ty in enumerate(("L", "R")):
                            DMA("sp", qv[par][(ty, 0)][64:66, :], qaug_b[h, ai], [], [r_qv[par][(ty, 0)]])
                            DMA("sp", qv[par][(ty, 1)][0:2, :], qaug_b[h, ai], [], [r_qv[par][(ty, 1)]])
                    for tt in range(NTT):
                        T0 = tt * 512
                        b = nextbank([0, 1, 2, 3])
                        for dc in range(DC):
                            MM(ps[b][:], wB[:, dc, 128:256], xT[:, dc, T0:T0 + 512], dc == 0, dc == DC - 1, [rwBs[1], r_xT[tt]], [r_ps[b]])
                        CP("act", kTm[0][0:64, T0:T0 + 512], ps[b][0:64, :], [r_ps[b]], [r_k[tt]])
                        CP("dve", kTm[1][64:128, T0:T0 + 512], ps[b][64:128, :], [r_ps[b]], [r_k[tt]])
                        b = nextbank([0, 1, 2, 3])
                        for sub in range(4):
                            for dc in range(DC):
                                MM(ps[b][:, sub * 128:(sub + 1) * 128], xT[:, dc, T0 + sub * 128:T0 + (sub + 1) * 128], wB[:, dc, 256:384],
                                   (dc == 0 and sub == 0), dc == DC - 1, [rwBs[2], r_xT[tt]], [r_ps[b]])
                        CP("dve", vB[:, tt * 4:tt * 4 + 4, :], ps[b][:].rearrange("p (s e) -> p s e", e=128), [r_ps[b]], [r_v[tt]])
                        b = nextbank([0, 1, 2, 3])
                        for dc in range(DC):
                            MM(ps[b][:], wB[:, dc, 0:128], xT[:, dc, T0:T0 + 512], dc == 0, dc == DC - 1, [rwBs[0], r_xT[tt]], [r_ps[b]])
                        ACTV(qTa[:, T0:T0 + 512], ps[b][:], AF.Identity, [r_ps[b]], [r_qT[tt]], scale=0.125)
                        if tt == min(1, NTT - 1) and pending[0] is not None:
                            pending[0](nextbank([0, 1, 2, 3]))
                            pending[0] = None
                    pti = 0

                    def qproj(qb_, b_):
                        par_ = qb_ % 2
                        types = ["N"] + (["L"] if qb_ > 0 else []) + (["R"] if qb_ < NTT - 1 else [])
                        for n_, ty in enumerate(types):
                            CP("dve", qv[par_][(ty, 0)][0:64, :], qTa[0:64, qb_ * 512:(qb_ + 1) * 512], [r_qT[qb_]], [r_qv[par_][(ty, 0)]])
                            CP("pool", qv[par_][(ty, 1)][64:128, :], qTa[64:128, qb_ * 512:(qb_ + 1) * 512], [r_qT[qb_]], [r_qv[par_][(ty, 1)]])

                    qproj(0, 3)
                    for qb in range(NTT):
                        Q0 = qb * 512
                        par = qb % 2
                        bo = [4, 5]
                        bz = [6, 7]

                        def qk(kc):
                            if kc < 4 * qb:
                                ty = "L"
                            elif kc < 4 * qb + 4:
                                ty = "N"
                            else:
                                ty = "R"
                            bs_ = []
                            for c in range(2):
                                b2 = 2 * (kc % 2) + c
                                MM(ps[b2][:], kTm[c][:, kc * 128:(kc + 1) * 128], qv[par][(ty, c)], True, True, [r_k[kc // 4], r_qv[par][(ty, c)]], [r_ps[b2]])
                                bs_.append(b2)
                            return bs_

                        def rest(kc, bs_, pti):
                            for c in range(2):
                                pt = (pti + c) % 6
                                if kc < 4 * qb:
                                    m = 4 * qb - kc
                                    ACTV(PT[pt], ps[bs_[c]][:], AF.Exp, [r_ps[bs_[c]], r_c], [r_PT[pt]], bias=bcol[:, h, m:m + 1])
                                elif kc >= 4 * qb + 4:
                                    m = kc - 4 * qb
                                    ACTV(PT[pt], ps[bs_[c]][:], AF.Exp, [r_ps[bs_[c]], r_c], [r_PT[pt]], bias=bcol[:, h, 32 + m:33 + m])
                                else:
                                    jd = kc - 4 * qb
                                    pf = fA[c][:, 0:512]
                                    ACTV(pf, ps[bs_[c]][:], AF.Exp, [r_ps[bs_[c]]], [r_pf[c]])
                                    TT("dve", PT[pt], pf, dt_sb[jd // 2][:, (jd % 2) * 512:(jd % 2 + 1) * 512], ALU.mult,
                                       [r_pf[c], r_fA[4 + jd // 2]], [r_PT[pt]])
                                MM(ps[bo[c]][:], vB[:, kc, :], PT[pt], kc == 0, kc == NKC - 1, [r_v[kc // 4], r_PT[pt]], [r_ps[bo[c]]])
                                MM(ps[bz[c]][:], ones_bf[:], PT[pt], kc == 0, kc == NKC - 1, [r_c, r_PT[pt]], [r_ps[bz[c]]])

                        prev = qk(0)
                        for kc in range(NKC):
                            nxt = qk(kc + 1) if kc + 1 < NKC else None
                            rest(kc, prev, pti)
                            if kc == min(10, NKC - 1) and pending[0] is not None:
                                pending[0](2 * (kc % 2))
                                pending[0] = None
                            if kc == min(2, NKC - 1) and qb + 1 < NTT:
                                qproj(qb + 1, None)
                            pti += 2
                            prev = nxt
                        r0, o0, r1, t1 = fA[0][:, 512:1024], fA[1][:, 512:1024], fA[2][:, 0:512], fA[3][:, 0:512]
                        CP("act", r0, ps[bz[0]][:], [r_ps[bz[0]]], [r_fA[0]])
                        CP("dve", o0, ps[bo[0]][:], [r_ps[bo[0]]], [r_fA[1]])
                        CP("act", r1, ps[bz[1]][:], [r_ps[bz[1]]], [r_fA[2]])
                        CP("dve", t1, ps[bo[1]][:], [r_ps[bo[1]]], [r_fA[3]])
                        P.emit("dve", lambda e, r0=r0: e.reciprocal(r0, r0), [r_fA[0]], [r_fA[0]])
                        TT("dve", o0, o0, r0, ALU.mult, [r_fA[0], r_fA[1]], [r_fA[1]])
                        P.emit("dve", lambda e, r1=r1: e.reciprocal(r1, r1), [r_fA[2]], [r_fA[2]])
                        TT("dve", t1, t1, r1, ALU.mult, [r_fA[2], r_fA[3]], [r_fA[3]])
                        STT("dve", o0, t1, neglam[:, l:l + 1], o0, ALU.mult, ALU.add, [r_fA[3], r_fA[1], r_c], [r_fA[1]])
                        TT("dve", sqb, o0, o0, ALU.mult, [r_fA[1]], [r_sq])
                        def tail(b, h=h, l=l, Q0=Q0, o0=o0, r1=r1):
                            MM(ps[b][:], ones_bf[:], sqb, True, True, [r_c, r_sq], [r_ps[b]])
                            TS("dve", r1, ps[b][:], 1.0 / 128.0, EPS, ALU.mult, ALU.add, [r_ps[b]], [r_fA[2]])
                            ACTV(r1, r1, AF.Ln, [r_fA[2]], [r_fA[2]])
                            ACTV(r1, r1, AF.Exp, [r_fA[2]], [r_fA[2]], scale=-0.5)
                            STT("dve", ob, o0, gsub[:, l:l + 1], r1, ALU.mult, ALU.mult, [r_fA[1], r_fA[2], r_c], [r_ob])
                            DMA("pool", mix_b[:, 2 + h, Q0:Q0 + 512], ob, [r_ob], [R()])

                        pending[0] = tail
                if pending[0] is not None:
                    pending[0](0)
                    pending[0] = None
                P.barrier()
                MARKS.append(('B', sum(1 for o_ in P.ops['pe'] if o_.fn is not None)))
                _chk('B')

                wo = reg[:, 0:8192].rearrange("p (f n) -> p f n", n=1024)
                mt = [reg[:, 8192 + i * 4096: 8192 + (i + 1) * 4096].rearrange("p (f t) -> p f t", t=512) for i in range(2)]
                r_wo, r_mt = R(), [R(), R()]
                DMA("sp", wo, wo_b[l], [], [r_wo])
                DMA("sp", lnp[:], lnp_d[l, 0], [], [r_lnp])
                pendT = []
                for tt in range(NTT):
                    T0 = tt * 512
                    m_ = mt[tt % 2]
                    DMA("sp", m_, mix_b[:, :, T0:T0 + 512], [], [r_mt[tt % 2]])
                    for sub in range(4):
                        t0 = T0 + sub * 128
                        sl = sub % 2
                        DMA("sp", xres[sl][:], xin_d[tok0 + t0: tok0 + t0 + 128, :], [], [r_xres[sl]])
                        yb = [nextbank(list(range(8))), nextbank(list(range(8)))]
                        for hf in range(2):
                            for fc in range(8):
                                MM(ps[yb[hf]][:], m_[:, fc, sub * 128:(sub + 1) * 128], wo[:, fc, hf * 512:(hf + 1) * 512],
                                   fc == 0, fc == 7, [r_mt[tt % 2], r_wo], [r_ps[yb[hf]]])
                        s3 = (tt * 4 + sub) % 3
                        pendT.append(ln_epilogue(yb, xres[sl], r_xres[sl], xa_d[tok0 + t0: tok0 + t0 + 128, :], t0, True, s3, 3 + s3))
                        if len(pendT) > 2:
                            pendT.pop(0)()
                while pendT:
                    pendT.pop(0)()
                P.barrier()
                MARKS.append(('O', sum(1 for o_ in P.ops['pe'] if o_.fn is not None)))
                _chk('O')

                hT = reg[:, 0:NFC * 512].rearrange("p (f t) -> p f t", t=512)
                wd = reg[:, NFC * 512: NFC * 512 + NFC * 1024].rearrange("p (f n) -> p f n", n=1024)
                assert NFC * 512 + NFC * 1024 <= REG
                r_hT, r_wd = R(), R()
                DMA("sp", wd, wd_b[l], [], [r_wd])
                DMA("sp", lnp[:], lnp_d[l, 1], [], [r_lnp])
                gi = 0
                pendF = []
                for tt in range(NTT):
                    T0 = tt * 512
                    for fg in range(NFC // 2):
                        ws = wring[gi % 3]
                        rws = r_wring[gi % 3]
                        gi += 1
                        DMA("sp", ws[:, :, 0:256], wg_b[l][:, :, fg * 256:(fg + 1) * 256], [], [rws[0]])
                        DMA("sp", ws[:, :, 256:512], wu_b[l][:, :, fg * 256:(fg + 1) * 256], [], [rws[1]])
                        if fg == 1 and pendF:
                            pendF.pop(0)()
                        for f2 in range(2):
                            fc = fg * 2 + f2
                            bg = nextbank([0, 1, 2, 3])
                            for dc in range(DC):
                                MM(ps[bg][:], ws[:, dc, f2 * 128:(f2 + 1) * 128], xT[:, dc, T0:T0 + 512], dc == 0, dc == DC - 1, [rws[0], r_xT[tt]], [r_ps[bg]])
                            bu = nextbank([0, 1, 2, 3])
                            for dc in range(DC):
                                MM(ps[bu][:], ws[:, dc, 256 + f2 * 128: 256 + (f2 + 1) * 128], xT[:, dc, T0:T0 + 512], dc == 0, dc == DC - 1, [rws[1], r_xT[tt]], [r_ps[bu]])
                            k_ = fc % 2
                            th = fA[4 + k_][:, 0:512]
                            s_ = fA[4 + k_][:, 512:1024]
                            ACTV(th, ps[bg][:], AF.Tanh, [r_ps[bg]], [r_fA[4 + k_]], scale=0.5)
                            STT("dve", s_, th, 1.0, ps[bg][:], ALU.add, ALU.mult, [r_fA[4 + k_], r_ps[bg]], [r_fA[4 + k_]])
                            STT("dve", hT[:, fc, :], s_, 0.5, ps[bu][:], ALU.mult, ALU.mult, [r_fA[4 + k_], r_ps[bu]], [r_hT])
                    for sub in range(4):
                        t0 = T0 + sub * 128
                        sl = sub % 2
                        DMA("sp", xres[sl][:], xa_d[tok0 + t0: tok0 + t0 + 128, :], [], [r_xres[sl]])
                        yb = [nextbank([4, 5, 6, 7]), nextbank([4, 5, 6, 7])]
                        for hf in range(2):
                            for fc in range(NFC):
                                MM(ps[yb[hf]][:], hT[:, fc, sub * 128:(sub + 1) * 128], wd[:, fc, hf * 512:(hf + 1) * 512],
                                   fc == 0, fc == NFC - 1, [r_hT, r_wd], [r_ps[yb[hf]]])
                        cl = ln_epilogue(yb, xres[sl], r_xres[sl], xout_d[tok0 + t0: tok0 + t0 + 128, :], t0, not last, sl, 2 + sl)
                        if pendF:
                            pendF.pop(0)()
                        if cl is not None:
                            pendF.append(cl)
                while pendF:
                    pendF.pop(0)()
                P.barrier()
                MARKS.append(('F', sum(1 for o_ in P.ops['pe'] if o_.fn is not None)))
            tok0 += S

    except _Stop:
        pass

    P.barrier()
    P.finalize()
    with ExitStack() as es2:
        engsem = {e: es2.enter_context(nc.semaphore(f"sem_{e}")) for e in Prog.ENGS}
        dmasem = {e: [es2.enter_context(nc.semaphore(f"dsem_{e}{i}")) for i in range(Prog.NSLOT)] for e in ("sp", "pool")}
        block = es2.enter_context(nc.Block())

        @block.tensor
        def _(e):
            P.replay("pe", e, engsem, dmasem)

        @block.scalar
        def _(e):
            P.replay("act", e, engsem, dmasem)

        @block.vector
        def _(e):
            P.replay("dve", e, engsem, dmasem)

        @block.gpsimd
        def _(e):
            P.replay("pool", e, engsem, dmasem)

        @block.sync
        def _(e):
            P.replay("sp", e, engsem, dmasem)
    es.close()
    return nc


def prep_shared(inp, seq_lens):
    f = np.float32
    plans, tiles = na_tiles(seq_lens)
    nt = tiles.shape[0]
    sh = {}
    for k in ("w_in", "w_out", "w_gate", "w_up", "w_down"):
        sh[k] = np.ascontiguousarray(inp[k], dtype=f)
    g = np.asarray(inp["a_ln_g"], f)
    b = np.asarray(inp["a_ln_b"], f)
    algb = np.stack([np.tile(g, (1, 2)), np.tile(b, (1, 2))], axis=1)
    sh["algb"] = np.ascontiguousarray(np.broadcast_to(algb[:, None], (DEPTH, 128, 2, 512)))
    ws = np.asarray(inp["a_w_s"], f)
    sh["wsT"] = np.ascontiguousarray(ws.transpose(0, 3, 1, 2))
    sh["bs"] = np.ascontiguousarray(np.asarray(inp["a_b_s"], f).reshape(DEPTH, 1, 512))
    lam = np.stack([inp["b_lambda_q1"], inp["b_lambda_k1"], inp["b_lambda_q2"], inp["b_lambda_k2"]], axis=1).astype(f)
    sh["lamp"] = np.ascontiguousarray(np.broadcast_to(lam[:, None], (DEPTH, 128, 4, 64)))
    sh["subg"] = np.ascontiguousarray(np.asarray(inp["b_subln_g"], f).reshape(DEPTH, 128, 1))
    rpb = np.asarray(inp["c_rpb"], f).reshape(DEPTH, 4, 15 * 31)
    cb = np.full((DEPTH, 128, nt, 4, 128), NEG, f)
    msk = tiles >= 0
    idx = np.where(msk, tiles, 0)
    for l in range(DEPTH):
        for h in range(4):
            v = rpb[l, h][idx]
            cb[l, :, :, h, :] = np.where(msk, v, f(NEG)).transpose(1, 0, 2)
    sh["cb"] = cb
    lg = np.asarray(inp["ln_g"], f)
    lb = np.asarray(inp["ln_b"], f)
    lnp = np.stack([lg, lb], axis=2)
    sh["lnp"] = np.ascontiguousarray(np.broadcast_to(lnp[:, :, None], (DEPTH, 2, 128, 2, D)))
    bcol, qaug, dtab = host_consts()
    sh["bcol"], sh["qaug"], sh["dtab"] = bcol, qaug, dtab
    sh["ident"] = np.eye(128, dtype=f)
    return sh, nt


_CACHE = {}


def kernel(**inp):
    seq_lens = [2048, 4096, 4096]
    n = 8
    sh, nt = prep_shared(inp, seq_lens)
    xp = np.asarray(inp["x_prompt"], np.float32)
    xs = np.asarray(inp["x_sample"], np.float32)
    key = (tuple(seq_lens), nt)
    if key not in _CACHE:
        _CACHE[key] = build(seq_lens, nt)
    nc = _CACHE[key]
    in_maps = []
    for c in range(n):
        xc = np.concatenate([xp[c], xs[2 * c], xs[2 * c + 1]], axis=0)
        m = dict(sh)
        m["x"] = np.ascontiguousarray(xc)
        in_maps.append(m)
    res = run_bass_kernel_spmd(nc, in_maps, core_ids=list(range(n)))
    yp = np.empty_like(xp)
    ys = np.empty_like(xs)
    for c in range(n):
        y = np.asarray(res.results[c]["y"], np.float32)
        yp[c] = y[0:2048]
        ys[2 * c] = y[2048:6144]
        ys[2 * c + 1] = y[6144:10240]
    return (yp, ys)
```
